# Optimizing a Trainium2 kernel written in Bass

```python
import math
import jax, jax.numpy as jnp
from jax import lax
import numpy as np

D_MODEL = 2048
BATCH = 4
SEQ = 2048
DEPTH = 4
DEC_BATCH = 2
DEC_SEQ = 16384
PAST_LEN = 128

GRID_W = 64
NA_HEADS = 16
NA_HEAD_DIM = 64
NA_WIDTH = NA_HEADS * NA_HEAD_DIM
NA_KH_MAX = 8
NA_KW = 16
LRU_WIDTH = 512
LRU_BLOCKS = 8
LRU_BLOCK_DIM = LRU_WIDTH // LRU_BLOCKS
LRU_CONV = 4
LRU_C = 8.0
S5_WIDTH = 512
S5_GROUP = 16
S5_GROUPS = S5_WIDTH // S5_GROUP
S5_STATE = 64
MIX_WIDTH = NA_WIDTH + LRU_WIDTH + S5_WIDTH
IN_WIDTH = 3 * NA_WIDTH + 2 * LRU_WIDTH + S5_WIDTH
D_FF = 4 * D_MODEL
EPS = 1e-6

kernel_name = 'hybrid_natten_rglru_s5_encoder'


def _rmsnorm(x, g):
    xf = x.astype(jnp.float32)
    y = xf * lax.rsqrt(jnp.mean(xf * xf, axis=-1, keepdims=True) + EPS)
    return (y * g.astype(jnp.float32)).astype(x.dtype)


def _neighbourhood_attention(q, k, v, rpb):
    b, t, _ = q.shape
    rows = t // GRID_W
    kh = min(NA_KH_MAX, rows)
    shp = (b, rows, GRID_W, NA_HEADS, NA_HEAD_DIM)
    q = q.reshape(shp) * (NA_HEAD_DIM ** -0.5)
    k = k.reshape(shp)
    v = v.reshape(shp)
    cols = np.arange(GRID_W)
    col_start = np.clip(cols - NA_KW // 2, 0, GRID_W - NA_KW)
    col_idx = col_start[:, None] + np.arange(NA_KW)[None, :]
    dc_idx = col_idx - cols[:, None] + (NA_KW - 1)
    bias_cols = rpb[:, :, dc_idx]

    def row_block(r):
        rs = jnp.clip(r - kh // 2, 0, rows - kh)
        k_rows = lax.dynamic_slice_in_dim(k, rs, kh, axis=1)
        v_rows = lax.dynamic_slice_in_dim(v, rs, kh, axis=1)
        k_win = k_rows[:, :, col_idx]
        v_win = v_rows[:, :, col_idx]
        q_r = lax.dynamic_index_in_dim(q, r, axis=1, keepdims=False)
        s = jnp.einsum('bwhd,bawkhd->bwhak', q_r, k_win).astype(jnp.float32)
        dr = rs + jnp.arange(kh) - r + (NA_KH_MAX - 1)
        bias = jnp.take(bias_cols, dr, axis=1)
        s = s + jnp.transpose(bias, (2, 0, 1, 3)).astype(jnp.float32)[None]
        p = jax.nn.softmax(s.reshape(b, GRID_W, NA_HEADS, kh * NA_KW), axis=-1)
        p = p.reshape(b, GRID_W, NA_HEADS, kh, NA_KW).astype(v.dtype)
        return jnp.einsum('bwhak,bawkhd->bwhd', p, v_win)

    out = lax.map(row_block, jnp.arange(rows))
    return jnp.transpose(out, (1, 0, 2, 3, 4)).reshape(b, t, NA_WIDTH)


def _linear_scan(a, u, reverse):
    def combine(e1, e2):
        a1, b1 = e1
        a2, b2 = e2
        return a1 * a2, a2 * b1 + b2
    return lax.associative_scan(combine, (a, u), reverse=reverse, axis=1)[1]


def _rglru_direction(xc, w_a, b_a, w_x, b_x, lam, reverse):
    bsz, t, _ = xc.shape
    xb = xc.reshape(bsz, t, LRU_BLOCKS, LRU_BLOCK_DIM)
    r = jax.nn.sigmoid(jnp.einsum('btnj,njk->btnk', xb, w_a).reshape(bsz, t, LRU_WIDTH) + b_a)
    i = jax.nn.sigmoid(jnp.einsum('btnj,njk->btnk', xb, w_x).reshape(bsz, t, LRU_WIDTH) + b_x)
    log_a = LRU_C * r.astype(jnp.float32) * jax.nn.log_sigmoid(lam.astype(jnp.float32))
    a = jnp.exp(log_a)
    mult = jnp.sqrt(-jnp.expm1(2.0 * log_a))
    u = mult * (i * xc).astype(jnp.float32)
    return _linear_scan(a, u, reverse)


def _rglru_branch(xr, gate, conv_w, conv_b, w_a, b_a, w_x, b_x, lam):
    t = xr.shape[1]
    left = LRU_CONV // 2
    xp = jnp.pad(xr, ((0, 0), (left, LRU_CONV - 1 - left), (0, 0)))
    xc = sum(xp[:, j:j + t] * conv_w[j] for j in range(LRU_CONV)) + conv_b
    h = (_rglru_direction(xc, w_a[0], b_a[0], w_x[0], b_x[0], lam[0], False)
         + _rglru_direction(xc, w_a[1], b_a[1], w_x[1], b_x[1], lam[1], True))
    return h.astype(xr.dtype) * jax.nn.gelu(gate)


def _s5_direction(u, a_re, a_im, log_dt, b_re, b_im, c_re, c_im, reverse):
    f32 = jnp.float32
    a_re = a_re.astype(f32)
    a_im = a_im.astype(f32)
    b_re = b_re.astype(f32)
    b_im = b_im.astype(f32)
    dt = jnp.exp(log_dt.astype(f32))[:, None]
    mag = jnp.exp(a_re * dt)
    lb_re = mag * jnp.cos(a_im * dt)
    lb_im = mag * jnp.sin(a_im * dt)
    den = a_re * a_re + a_im * a_im
    n_re = lb_re - 1.0
    n_im = lb_im
    co_re = (n_re * a_re + n_im * a_im) / den
    co_im = (n_im * a_re - n_re * a_im) / den
    bb_re = co_re[..., None] * b_re - co_im[..., None] * b_im
    bb_im = co_re[..., None] * b_im + co_im[..., None] * b_re
    in_re = jnp.einsum('btgh,gph->btgp', u, bb_re)
    in_im = jnp.einsum('btgh,gph->btgp', u, bb_im)
    ar = jnp.broadcast_to(lb_re, in_re.shape)
    ai = jnp.broadcast_to(lb_im, in_re.shape)

    def combine(e1, e2):
        ar1, ai1, br1, bi1 = e1
        ar2, ai2, br2, bi2 = e2
        return (ar1 * ar2 - ai1 * ai2, ar1 * ai2 + ai1 * ar2,
                ar2 * br1 - ai2 * bi1 + br2, ar2 * bi1 + ai2 * br1 + bi2)

    _, _, s_re, s_im = lax.associative_scan(combine, (ar, ai, in_re, in_im), reverse=reverse, axis=1)
    return (jnp.einsum('btgp,ghp->btgh', s_re, c_re.astype(f32))
            - jnp.einsum('btgp,ghp->btgh', s_im, c_im.astype(f32)))


def _s5_branch(xs, a_re, a_im, log_dt, b_re, b_im, c_re, c_im, d, w_glu, b_glu):
    bsz, t, _ = xs.shape
    xf = xs.astype(jnp.float32)
    u = xf.reshape(bsz, t, S5_GROUPS, S5_GROUP)
    y = (_s5_direction(u, a_re[0], a_im[0], log_dt[0], b_re[0], b_im[0], c_re[0], c_im[0], False)
         + _s5_direction(u, a_re[1], a_im[1], log_dt[1], b_re[1], b_im[1], c_re[1], c_im[1], True))
    y = y.reshape(bsz, t, S5_WIDTH) + d.astype(jnp.float32) * xf
    y = jax.nn.gelu(y).astype(xs.dtype)
    return y * jax.nn.sigmoid(y @ w_glu + b_glu)


def _mixer(h, w_in, na_rpb, lru_conv_w, lru_conv_b, lru_w_a, lru_b_a, lru_w_x, lru_b_x, lru_lambda,
           s5_a_re, s5_a_im, s5_log_dt, s5_b_re, s5_b_im, s5_c_re, s5_c_im, s5_d, s5_w_glu, s5_b_glu,
           g_out, w_out):
    z = h @ w_in
    q, k, v, xr, gate, xs = jnp.split(
        z, [NA_WIDTH, 2 * NA_WIDTH, 3 * NA_WIDTH, 3 * NA_WIDTH + LRU_WIDTH, 3 * NA_WIDTH + 2 * LRU_WIDTH], axis=-1)
    ya = _neighbourhood_attention(q, k, v, na_rpb)
    yb = _rglru_branch(xr, gate, lru_conv_w, lru_conv_b, lru_w_a, lru_b_a, lru_w_x, lru_b_x, lru_lambda)
    yc = _s5_branch(xs, s5_a_re, s5_a_im, s5_log_dt, s5_b_re, s5_b_im, s5_c_re, s5_c_im, s5_d, s5_w_glu, s5_b_glu)
    ga, gb, gc = jnp.split(g_out, [NA_WIDTH, NA_WIDTH + LRU_WIDTH])
    y = jnp.concatenate([_rmsnorm(ya, ga), _rmsnorm(yb, gb), _rmsnorm(yc, gc)], axis=-1)
    return y @ w_out


def _trunk(x, norm_mix_g, w_in, na_rpb, lru_conv_w, lru_conv_b, lru_w_a, lru_b_a, lru_w_x, lru_b_x,
           lru_lambda, s5_a_re, s5_a_im, s5_log_dt, s5_b_re, s5_b_im, s5_c_re, s5_c_im, s5_d, s5_w_glu,
           s5_b_glu, g_out, w_out, norm_mlp_g, w_mlp_up, w_mlp_down, final_g):
    for l in range(DEPTH):
        h = _rmsnorm(x, norm_mix_g[l])
        x = x + _mixer(h, w_in[l], na_rpb[l], lru_conv_w[l], lru_conv_b[l], lru_w_a[l], lru_b_a[l],
                       lru_w_x[l], lru_b_x[l], lru_lambda[l], s5_a_re[l], s5_a_im[l], s5_log_dt[l],
                       s5_b_re[l], s5_b_im[l], s5_c_re[l], s5_c_im[l], s5_d[l], s5_w_glu[l], s5_b_glu[l],
                       g_out[l], w_out[l])
        h = _rmsnorm(x, norm_mlp_g[l])
        x = x + jnp.square(jax.nn.relu(h @ w_mlp_up[l])) @ w_mlp_down[l]
    return _rmsnorm(x, final_g)


def setup_inputs(seed: int = 0) -> dict:
    key = jax.random.key(seed)
    ks = jax.random.split(key, 32)
    f32 = jnp.float32
    L = DEPTH

    def nrm(k, shape, scale):
        return jax.random.normal(k, shape, f32) * scale

    x_prompt = nrm(ks[0], (BATCH, SEQ, D_MODEL), 1.0)
    x_sample = nrm(ks[1], (DEC_BATCH, DEC_SEQ, D_MODEL), 1.0)
    norm_mix_g = 1.0 + nrm(ks[2], (L, D_MODEL), 0.02)
    w_in = nrm(ks[3], (L, D_MODEL, IN_WIDTH), D_MODEL ** -0.5)
    na_rpb = nrm(ks[4], (L, NA_HEADS, 2 * NA_KH_MAX - 1, 2 * NA_KW - 1), 0.1)
    lru_conv_w = nrm(ks[5], (L, LRU_CONV, LRU_WIDTH), LRU_CONV ** -0.5)
    lru_conv_b = nrm(ks[6], (L, LRU_WIDTH), 0.01)
    lru_w_a = nrm(ks[7], (L, 2, LRU_BLOCKS, LRU_BLOCK_DIM, LRU_BLOCK_DIM), LRU_BLOCK_DIM ** -0.5)
    lru_b_a = nrm(ks[8], (L, 2, LRU_WIDTH), 0.01)
    lru_w_x = nrm(ks[9], (L, 2, LRU_BLOCKS, LRU_BLOCK_DIM, LRU_BLOCK_DIM), LRU_BLOCK_DIM ** -0.5)
    lru_b_x = nrm(ks[10], (L, 2, LRU_WIDTH), 0.01)
    a_pow = jax.random.uniform(ks[11], (L, 2, LRU_WIDTH), f32, 0.9, 0.999)
    s_l = a_pow ** (1.0 / LRU_C)
    lru_lambda = jnp.log(s_l) - jnp.log1p(-s_l)
    n_idx = jnp.arange(S5_STATE, dtype=f32)
    s5_a_re = -0.5 + nrm(ks[12], (L, 2, S5_GROUPS, S5_STATE), 0.01)
    s5_a_im = math.pi * n_idx + nrm(ks[13], (L, 2, S5_GROUPS, S5_STATE), 0.01)
    s5_log_dt = jax.random.uniform(ks[14], (L, 2, S5_GROUPS), f32, math.log(1e-3), math.log(1e-1))
    s5_b_re = nrm(ks[15], (L, 2, S5_GROUPS, S5_STATE, S5_GROUP), (2 * S5_GROUP) ** -0.5)
    s5_b_im = nrm(ks[16], (L, 2, S5_GROUPS, S5_STATE, S5_GROUP), (2 * S5_GROUP) ** -0.5)
    s5_c_re = nrm(ks[17], (L, 2, S5_GROUPS, S5_GROUP, S5_STATE), (2 * S5_STATE) ** -0.5)
    s5_c_im = nrm(ks[18], (L, 2, S5_GROUPS, S5_GROUP, S5_STATE), (2 * S5_STATE) ** -0.5)
    s5_d = nrm(ks[19], (L, S5_WIDTH), 1.0)
    s5_w_glu = nrm(ks[20], (L, S5_WIDTH, S5_WIDTH), S5_WIDTH ** -0.5)
    s5_b_glu = nrm(ks[21], (L, S5_WIDTH), 0.01)
    g_out = 1.0 + nrm(ks[22], (L, MIX_WIDTH), 0.02)
    w_out = nrm(ks[23], (L, MIX_WIDTH, D_MODEL), MIX_WIDTH ** -0.5)
    norm_mlp_g = 1.0 + nrm(ks[24], (L, D_MODEL), 0.02)
    w_mlp_up = nrm(ks[25], (L, D_MODEL, D_FF), D_MODEL ** -0.5)
    w_mlp_down = nrm(ks[26], (L, D_FF, D_MODEL), D_FF ** -0.5)
    final_g = 1.0 + nrm(ks[27], (D_MODEL,), 0.02)
    return {'x_prompt': x_prompt, 'x_sample': x_sample, 'norm_mix_g': norm_mix_g, 'w_in': w_in,
            'na_rpb': na_rpb, 'lru_conv_w': lru_conv_w, 'lru_conv_b': lru_conv_b, 'lru_w_a': lru_w_a,
            'lru_b_a': lru_b_a, 'lru_w_x': lru_w_x, 'lru_b_x': lru_b_x, 'lru_lambda': lru_lambda,
            's5_a_re': s5_a_re, 's5_a_im': s5_a_im, 's5_log_dt': s5_log_dt, 's5_b_re': s5_b_re,
            's5_b_im': s5_b_im, 's5_c_re': s5_c_re, 's5_c_im': s5_c_im, 's5_d': s5_d,
            's5_w_glu': s5_w_glu, 's5_b_glu': s5_b_glu, 'g_out': g_out, 'w_out': w_out,
            'norm_mlp_g': norm_mlp_g, 'w_mlp_up': w_mlp_up, 'w_mlp_down': w_mlp_down, 'final_g': final_g}


def reference(x_prompt, x_sample, norm_mix_g, w_in, na_rpb, lru_conv_w, lru_conv_b, lru_w_a, lru_b_a,
              lru_w_x, lru_b_x, lru_lambda, s5_a_re, s5_a_im, s5_log_dt, s5_b_re, s5_b_im, s5_c_re,
              s5_c_im, s5_d, s5_w_glu, s5_b_glu, g_out, w_out, norm_mlp_g, w_mlp_up, w_mlp_down, final_g):
    weights = (norm_mix_g, w_in, na_rpb, lru_conv_w, lru_conv_b, lru_w_a, lru_b_a, lru_w_x, lru_b_x,
               lru_lambda, s5_a_re, s5_a_im, s5_log_dt, s5_b_re, s5_b_im, s5_c_re, s5_c_im, s5_d,
               s5_w_glu, s5_b_glu, g_out, w_out, norm_mlp_g, w_mlp_up, w_mlp_down, final_g)
    y_prompt = _trunk(x_prompt, *weights)
    y_sample = _trunk(x_sample, *weights)
    return (y_prompt, y_sample)
```

```python
import numpy as np
from contextlib import ExitStack
import concourse.bass as bass
import concourse.mybir as mybir
from concourse.bass_utils import run_bass_kernel_spmd

F32 = mybir.dt.float32
BF16 = mybir.dt.bfloat16
I32 = mybir.dt.int32
AF = mybir.ActivationFunctionType
ALU = mybir.AluOpType
ENGS = ['sp', 'act', 'dve', 'pool', 'pe']
NCORES = 8
EPS = 1e-6
NEG = -30000.0
MAGIC = 12582912.0
TWO_PI = 6.283185307179586


class Buf:
    __slots__ = ('name', 'w', 'r', 'accum')

    def __init__(self, name='', accum=False):
        self.name = name
        self.w = {}
        self.r = {}
        self.accum = accum


def _b(x):
    return x.buf if hasattr(x, 'buf') else x


class Tile:
    def __init__(self, handle, name):
        self.h = handle
        self.buf = Buf(name)
        self.sem = None

    def __getitem__(self, k):
        return self.h[k]


class View:
    def __init__(self, tile, ap):
        self.h = ap
        self.buf = tile.buf
        self.tile = tile

    def __getitem__(self, k):
        return self.h[k]


class DT:
    def __init__(self, handle, name):
        self.h = handle
        self.buf = Buf(name, accum=True)
        self.regs = {}
        self.sem = None

    def __getitem__(self, k):
        return self.h[k]

    def reg(self, key):
        if key not in self.regs:
            self.regs[key] = Buf('%s/%s' % (self.buf.name, key), accum=True)
        return self.regs[key]

    def allregs(self):
        return [self.buf] + list(self.regs.values())


class Sched:
    def __init__(self, nc, es):
        self.nc = nc
        self.es = es
        self.sems = {}
        self.cnt = {}
        for e in ENGS[1:]:
            self._mksem(e)
        self.prog = {e: [] for e in ENGS}
        self.seen = {e: {} for e in ENGS}
        self.free_dma = []
        self.nd = 0
        self.ninst = 0

    def _mksem(self, key):
        self.sems[key] = self.es.enter_context(self.nc.semaphore('s_' + key))
        self.cnt[key] = 0

    def get_dma_sem(self):
        if self.free_dma:
            return self.free_dma.pop()
        key = 'd%d' % self.nd
        self.nd += 1
        self._mksem(key)
        return key

    def _wait(self, eng, toks):
        for key, val in toks.items():
            if val <= 0 or self.seen[eng].get(key, 0) >= val:
                continue
            if key == 'pe' and eng == 'pe':
                continue
            self.seen[eng][key] = val
            self.prog[eng].append(('wait', key, val))

    def _deps(self, eng, reads, writes):
        for b in reads:
            self._wait(eng, _b(b).w)
        for b in writes:
            bb = _b(b)
            if not bb.accum:
                self._wait(eng, bb.w)
            self._wait(eng, bb.r)

    def _mark(self, reads, writes, key, val):
        for b in reads:
            bb = _b(b)
            if bb.r.get(key, 0) < val:
                bb.r[key] = val
        for b in writes:
            bb = _b(b)
            if bb.accum:
                if bb.w.get(key, 0) < val:
                    bb.w[key] = val
            else:
                bb.w = {key: val}
                bb.r = {}

    def op(self, eng, fn, reads=(), writes=(), inc=True):
        self.ninst += 1
        self._deps(eng, reads, writes)
        if inc:
            self.cnt[eng] += 1
            self.prog[eng].append(('op', fn, eng, 1))
            self._mark(reads, writes, eng, self.cnt[eng])
        else:
            self.prog[eng].append(('op', fn, None, 0))
            self._mark(reads, writes, eng, self.cnt[eng] + 1)

    def dma(self, out, in_, dst, src=None, q='sp', extra_reads=(), fn=None, **kw):
        self.ninst += 1
        owner = dst[0] if isinstance(dst, tuple) else dst
        wbuf = dst[1] if isinstance(dst, tuple) else dst
        if isinstance(owner, DT) and isinstance(src, Tile):
            owner = src
        if owner.sem is None:
            owner.sem = self.get_dma_sem()
        sem = owner.sem
        reads = ([src] if src is not None else []) + list(extra_reads)
        self._deps(q, reads, [wbuf])
        self.cnt[sem] += 16
        if fn is None:
            fn = (lambda e, o=out, i=in_, k=kw: e.dma_start(out=o, in_=i, **k))
        self.prog[q].append(('op', fn, sem, 16))
        self._mark(reads, [wbuf], sem, self.cnt[sem])

    def custom(self, eng, fn, owner, n, reads=(), writes=()):
        if owner.sem is None:
            owner.sem = self.get_dma_sem()
        sem = owner.sem
        self._deps(eng, reads, writes)
        self.cnt[sem] += n
        self.prog[eng].append(('op', fn, sem, n))
        self._mark(reads, writes, sem, self.cnt[sem])

    def barrier(self):
        allt = dict(self.cnt)
        for e in ENGS:
            self._wait(e, allt)

    def emit(self):
        nc = self.nc
        prog = self.prog
        sems = self.sems

        def replay(eng_obj, items):
            for it in items:
                if it[0] == 'wait':
                    eng_obj.wait_ge(sems[it[1]], it[2])
                else:
                    ins = it[1](eng_obj)
                    if it[2] is not None:
                        ins.then_inc(sems[it[2]], it[3])
        with nc.Block() as block:
            @block.sync
            def _(e):
                replay(e, prog['sp'])

            @block.scalar
            def _(e):
                replay(e, prog['act'])

            @block.vector
            def _(e):
                replay(e, prog['dve'])

            @block.gpsimd
            def _(e):
                replay(e, prog['pool'])

            @block.tensor
            def _(e):
                replay(e, prog['pe'])
        self.prog = {e: [] for e in ENGS}


class Phase:
    def __init__(self, K, name):
        self.K = K
        self.S = K.S
        self.nc = K.nc
        self.name = name
        self.es = ExitStack()
        self.tiles = []
        self.n = 0

    def sb(self, shape, dt, name=None):
        self.n += 1
        nm = '%s_%s%d' % (self.name, name or 't', self.n)
        t = Tile(self.es.enter_context(self.nc.sbuf_tensor(nm, list(shape), dt)), nm)
        self.tiles.append(t)
        return t

    def ps(self, shape, dt, name=None):
        self.n += 1
        nm = '%s_%s%d' % (self.name, name or 'p', self.n)
        t = Tile(self.es.enter_context(self.nc.psum_tensor(nm, list(shape), dt)), nm)
        self.tiles.append(t)
        return t

    def close(self):
        self.S.barrier()
        self.S.emit()
        for t in self.tiles:
            if t.sem is not None:
                self.S.free_dma.append(t.sem)
                t.sem = None
        self.es.close()


class Ring:
    def __init__(self, items):
        self.items = items
        self.i = 0

    def next(self):
        t = self.items[self.i % len(self.items)]
        self.i += 1
        return t


def mkap(base, dims, off=0):
    return bass.AP(base.tensor, base.offset + off, [list(base.ap[0])] + [list(d) for d in dims])


class Cfg:
    def __init__(self, D=2048, DFF=8192, L=4, RS=64, RP=16, TT=1024):
        self.D, self.DFF, self.L, self.RS, self.RP, self.TT = D, DFF, L, RS, RP, TT
        self.DC = D // 128
        self.FC = DFF // 128
        self.HQ = 4
        self.FCQ = self.FC // self.HQ
        self.NS = RS * 64
        self.NP = RP * 64
        self.NT = self.NS + self.NP
        self.NH = TT // 512
        assert self.NS % TT == 0 and self.NP % TT == 0
        self.NTILES = self.NT // TT
        self.segs = [(0, self.NS, RS), (self.NS, self.NP, RP)]
        self.SEQ_S = 4 * self.NS
        self.SEQ_P = 2 * self.NP
        self.NSC = self.NT // 8
        self.J = 32
        self.NBLK = self.NSC // self.J
        self.segblk = [(0, self.NS // 256), (self.NS // 256, self.NP // 256)]


def pvec_layout(cfg):
    lay = {}
    off = 0

    def add(name, w):
        nonlocal off
        lay[name] = (off, w)
        off += w
    for l in range(cfg.L):
        add('g_mix%d' % l, cfg.DC)
        add('g_mlp%d' % l, cfg.DC)
        add('g_out%d' % l, 16)
        add('conv_w%d' % l, 16)
        add('conv_b%d' % l, 4)
        for d in range(2):
            add('b_a%d_%d' % (l, d), 4)
            add('b_x%d_%d' % (l, d), 4)
            add('lam%d_%d' % (l, d), 4)
        add('s5_d%d' % l, 4)
        add('b_glu%d' % l, 4)
    add('final_g', cfg.DC)
    lay['_n'] = off
    return lay


S5E_N = 29


def s5_exponents(cfg):
    e = np.zeros((128, S5E_N), np.float32)
    s = np.arange(8)
    for d in range(2):
        rows = slice(d * 64, d * 64 + 64)
        eb = (7 - s) if d == 0 else s
        ec = (s + 1) if d == 0 else (8 - s)
        e[rows, 0:8] = eb
        e[rows, 8:16] = ec
        e[rows, 16:24] = ec - 8
        e[rows, 24] = 1
        e[rows, 25] = 8
        e[rows, 26] = 256
        e[rows, 27] = cfg.NS
        e[rows, 28] = cfg.NP
    return e


def fm(v, nchunks):
    return np.ascontiguousarray(np.asarray(v, np.float32).reshape(nchunks, 128).T)


def blk_lhsT(w, KC):
    K, M = w.shape
    a = w.reshape(KC, 128, M // 128, 128)
    return np.ascontiguousarray(a.transpose(2, 1, 0, 3).reshape(M // 128, 128, KC * 128))


def blk_rhs(w, KC, NW=512):
    K, N = w.shape
    a = w.reshape(KC, 128, N // NW, NW)
    return np.ascontiguousarray(a.transpose(2, 1, 0, 3).reshape(N // NW, 128, KC * NW))


def host_layout(cfg, inp):
    L, D, DC, FC = cfg.L, cfg.D, cfg.DC, cfg.FC
    f32 = np.float32
    sh = {}
    w_in = np.asarray(inp['w_in'], f32)
    cols_f = np.r_[0:2048, 3072:4096]
    cols_t = np.r_[2048:3072, 4096:4608]
    sh['wf_in'] = np.concatenate([blk_lhsT(np.ascontiguousarray(w_in[l][:, cols_f]), DC) for l in range(L)], 0)
    sh['wt_in'] = np.concatenate([blk_rhs(np.ascontiguousarray(w_in[l][:, cols_t]), DC) for l in range(L)], 0)
    sh['wf_out'] = np.concatenate([blk_lhsT(np.asarray(inp['w_out'][l], f32), 16) for l in range(L)], 0)
    sh['wf_up'] = np.concatenate([blk_lhsT(np.asarray(inp['w_mlp_up'][l], f32), DC) for l in range(L)], 0)
    sh['wf_dn'] = np.concatenate([blk_lhsT(np.asarray(inp['w_mlp_down'][l], f32), FC) for l in range(L)], 0)
    sh['w_glu'] = np.concatenate([blk_lhsT(np.asarray(inp['s5_w_glu'][l], f32), 4) for l in range(L)], 0)
    wbd = np.zeros((L, 2, 2, 4, 128, 128), f32)
    for gi, nm in enumerate(['lru_w_a', 'lru_w_x']):
        w = np.asarray(inp[nm], f32)
        for c in range(4):
            wbd[:, :, gi, c, 0:64, 0:64] = w[:, :, 2 * c]
            wbd[:, :, gi, c, 64:128, 64:128] = w[:, :, 2 * c + 1]
    sh['lru_wbd'] = wbd.reshape(L * 16, 128, 128)
    lay = pvec_layout(cfg)
    pv = np.zeros((128, lay['_n']), f32)

    def put(name, arr):
        o, w = lay[name]
        pv[:, o:o + w] = arr
    for l in range(L):
        put('g_mix%d' % l, fm(inp['norm_mix_g'][l], DC))
        put('g_mlp%d' % l, fm(inp['norm_mlp_g'][l], DC))
        put('g_out%d' % l, fm(inp['g_out'][l], 16))
        cw = np.asarray(inp['lru_conv_w'][l], f32)
        put('conv_w%d' % l, np.stack([fm(cw[j], 4) for j in range(4)], -1).reshape(128, 16))
        put('conv_b%d' % l, fm(inp['lru_conv_b'][l], 4))
        for d in range(2):
            put('b_a%d_%d' % (l, d), fm(inp['lru_b_a'][l][d], 4))
            put('b_x%d_%d' % (l, d), fm(inp['lru_b_x'][l][d], 4))
            put('lam%d_%d' % (l, d), fm(inp['lru_lambda'][l][d], 4))
        put('s5_d%d' % l, fm(inp['s5_d'][l], 4))
        put('b_glu%d' % l, fm(inp['s5_b_glu'][l], 4))
    put('final_g', fm(inp['final_g'], DC))
    sh['pvec'] = pv
    s5p = np.zeros((L, 128, 96 + 4 * 512), f32)
    for l in range(L):
        for d in range(2):
            rows = slice(d * 64, d * 64 + 64)
            s5p[l, rows, 0:32] = np.asarray(inp['s5_a_re'][l][d], f32).T
            s5p[l, rows, 32:64] = np.asarray(inp['s5_a_im'][l][d], f32).T
            s5p[l, rows, 64:96] = np.asarray(inp['s5_log_dt'][l][d], f32)[None, :]
            s5p[l, rows, 96:608] = np.asarray(inp['s5_b_re'][l][d], f32).transpose(1, 0, 2).reshape(64, 512)
            s5p[l, rows, 608:1120] = np.asarray(inp['s5_b_im'][l][d], f32).transpose(1, 0, 2).reshape(64, 512)
            s5p[l, rows, 1120:1632] = np.asarray(inp['s5_c_re'][l][d], f32).transpose(2, 0, 1).reshape(64, 512)
            s5p[l, rows, 1632:2144] = np.asarray(inp['s5_c_im'][l][d], f32).transpose(2, 0, 1).reshape(64, 512)
    sh['s5p'] = s5p
    rpb = np.asarray(inp['na_rpb'], f32)
    w_ = np.arange(64)[:, None]
    kc_ = np.arange(64)[None, :]
    idx = np.clip(kc_ - w_ + 15, 0, 30)
    g = rpb[:, :, :, idx]
    g = g.transpose(0, 1, 3, 2, 4).reshape(L, 8, 2, 64, 15, 64).reshape(L * 8, 128, 15 * 64)
    sh['nag'] = np.ascontiguousarray(g)
    cs = np.clip(np.arange(64) - 8, 0, 48)
    valid = (kc_ >= cs[:, None]) & (kc_ < cs[:, None] + 16)
    m = np.where(valid, 0.0, NEG).astype(f32)
    sh['nacm'] = np.ascontiguousarray(np.concatenate([m, m], 0))
    sh['ident'] = np.eye(128, dtype=f32)
    sh['s5e'] = s5_exponents(cfg)
    cm = np.zeros((128, 2, 128), f32)
    s_i = np.arange(128)[:, None] // 16
    t_i = np.arange(128)[None, :] // 16
    cm[:, 0, :] = (t_i >= s_i)
    cm[:, 1, :] = (s_i >= t_i)
    sh['s5cm'] = cm
    dm = np.zeros((128, 2), f32)
    dm[0:64, 0] = 1
    dm[64:128, 1] = 1
    sh['dirmask'] = dm

    xs_ = np.asarray(inp['x_sample'], f32)
    xp_ = np.asarray(inp['x_prompt'], f32)
    in_maps = []
    for c in range(NCORES):
        m_ = dict(sh)
        sq, qq = c // 4, c % 4
        pq, hh = c // 2, c % 2
        m_['x_tok'] = np.ascontiguousarray(np.concatenate(
            [xs_[sq, qq * cfg.NS:(qq + 1) * cfg.NS], xp_[pq, hh * cfg.NP:(hh + 1) * cfg.NP]], 0))
        has = [qq > 0, qq < 3, hh > 0, hh < 1]
        nb = [c - 1, c + 1, c - 1, c + 1]
        m_['cidx'] = np.array([[nb[i] if has[i] else c for i in range(4)] + [0] * 4], np.int32)
        cmk = np.zeros((128, 4 + 32 + 192 + 32), f32)
        cmk[:, 0:4] = np.array(has, f32)[None]
        hm = np.zeros((2, 2, 8), f32)
        for r in range(8):
            hm[0, 0, r] = (r // 4 == sq and r < c)
            hm[0, 1, r] = (r // 4 == sq and r > c)
            hm[1, 0, r] = (r // 2 == pq and r < c)
            hm[1, 1, r] = (r // 2 == pq and r > c)
        cmk[:, 4:36] = hm.reshape(1, 32)
        em = np.full((2, 8, 12), NEG, f32)
        for si, (hp_, hn_, R) in enumerate([(has[0], has[1], cfg.RS), (has[2], has[3], cfg.RP)]):
            for r in range(4):
                for e in range(12):
                    loc = e - 4
                    ok = (-4 <= loc - r <= 3) if hp_ else (0 <= loc <= 7)
                    if ok and (loc >= 0 or hp_):
                        em[si, r, e] = 0.0
            for i in range(4):
                for j in range(12):
                    loc_rel = j - 4 - i
                    lrow = R - 8 + j
                    ok = (-4 <= loc_rel <= 3) if hn_ else (R - 8 <= lrow <= R - 1)
                    if ok and (lrow <= R - 1 or (hn_ and lrow <= R + 2)):
                        em[si, 4 + i, j] = 0.0
        cmk[:, 36:228] = em.reshape(1, 192)
        oh = np.zeros((4, 8), f32)
        for i in range(4):
            if has[i]:
                oh[i, nb[i]] = 1.0
        cmk[:, 228:260] = oh.reshape(1, 32)
        m_['cmask'] = cmk
        in_maps.append(m_)
    return in_maps


class Kern:
    def __init__(self, cfg, debug=(), stub_mixer=False):
        self.cfg = cfg
        self.debug = set(debug)
        self.stub_mixer = stub_mixer
        self.nc = bass.Bass("TRN2", target_bir_lowering=False)
        self.es = ExitStack()
        self.S = Sched(self.nc, self.es)
        self.lay = pvec_layout(cfg)
        self.inputs = {}
        self.scr = {}

    def din(self, name, shape, dt=F32):
        t = DT(self.nc.dram_tensor(name, list(shape), dt, kind="ExternalInput"), name)
        self.inputs[name] = t
        return t

    def dscr(self, name, shape, dt):
        if name in self.debug:
            t = DT(self.nc.dram_tensor(name, list(shape), dt, kind="ExternalOutput"), name)
        else:
            t = DT(self.nc.dram_tensor(name, list(shape), dt), name)
        self.scr[name] = t
        return t

    def declare(self):
        c = self.cfg
        L, D, DC, FC, NT = c.L, c.D, c.DC, c.FC, c.NT
        self.x_tok = self.din('x_tok', [NT, D])
        self.wsrc = {
            'in_f': self.din('wf_in', [L * 24, 128, DC * 128]),
            'in_t': self.din('wt_in', [L * 3, 128, DC * 512]),
            'out': self.din('wf_out', [L * DC, 128, 16 * 128]),
            'up': self.din('wf_up', [L * FC, 128, DC * 128]),
            'dn': self.din('wf_dn', [L * DC, 128, FC * 128]),
            'glu': self.din('w_glu', [L * 4, 128, 4 * 128]),
            'lru': self.din('lru_wbd', [L * 16, 128, 128]),
        }
        self.pvec_d = self.din('pvec', [128, self.lay['_n']])
        self.s5p_d = self.din('s5p', [L, 128, 2144])
        self.nag_d = self.din('nag', [L * 8, 128, 960])
        self.nacm_d = self.din('nacm', [128, 64])
        self.ident_d = self.din('ident', [128, 128])
        self.s5e_d = self.din('s5e', [128, S5E_N])
        self.s5cm_d = self.din('s5cm', [128, 2, 128])
        self.dirmask_d = self.din('dirmask', [128, 2])
        self.cidx_d = self.din('cidx', [1, 8], I32)
        self.cmask_d = self.din('cmask', [128, 260])
        self.y_tok = DT(self.nc.dram_tensor('y_tok', [NT, D], F32, kind="ExternalOutput"), 'y_tok')
        self.wb = {k: self.dscr('wb_' + k, list(v.h.shape), BF16) for k, v in self.wsrc.items()}
        self.xT = self.dscr('xT', [D, NT], F32)
        self.zT = self.dscr('zT', [3072, NT], BF16)
        self.vtok = self.dscr('vtok', [NT, 1024], BF16)
        self.xstok = self.dscr('xstok', [NT, 512], BF16)
        self.ymixraw = self.dscr('ymixraw', [2048, NT], F32)
        self.ymixT = self.dscr('ymixT', [2048, NT], BF16)
        self.natab = self.dscr('natab', [L * 8, 128, 960], F32)

    def load_consts(self, P, want=('ones', 'identf', 'identb', 'pvec')):
        S = self.S
        K = {}
        if 'pvec' in want:
            K['pvec'] = P.sb([128, self.lay['_n']], F32, 'pvec')
            S.dma(K['pvec'][:], self.pvec_d[:, :], K['pvec'], self.pvec_d)
        if 'identf' in want or 'identb' in want:
            K['identf'] = P.sb([128, 128], F32, 'identf')
            S.dma(K['identf'][:], self.ident_d[:, :], K['identf'], self.ident_d)
        if 'identb' in want:
            K['identb'] = P.sb([128, 128], BF16, 'identb')
            S.op('dve', lambda e: e.tensor_copy(out=K['identb'][:], in_=K['identf'][:]), [K['identf']], [K['identb']])
        if 'ones' in want:
            K['ones'] = P.sb([128, 128], BF16, 'ones')
            S.op('dve', lambda e: e.memset(K['ones'][:], 1.0), [], [K['ones']])
        return K

    def pv(self, K, name, c0=0, n=1):
        o, w = self.lay[name]
        return K['pvec'][:, o + c0:o + c0 + n]

    def prologue(self):
        S = self.S
        P = Phase(self, 'pro')
        FMAX = 8192
        stg = Ring([P.sb([128, FMAX], F32, 'stg') for _ in range(3)])
        stb = Ring([P.sb([128, FMAX], BF16, 'stb') for _ in range(3)])
        engs = Ring(['act', 'dve', 'pool', 'dve'])
        nem = 0
        for k, src in self.wsrc.items():
            dst = self.wb[k]
            NB, _, Fw = src.h.shape
            G = max(1, FMAX // Fw)
            for b0 in range(0, NB, G):
                g = min(G, NB - b0)
                a = stg.next()
                b = stb.next()
                av = a[:, 0:g * Fw].rearrange("p (g f) -> p g f", g=g)
                bv = b[:, 0:g * Fw].rearrange("p (g f) -> p g f", g=g)
                S.dma(av, src[b0:b0 + g, :, :].rearrange("g p f -> p g f"), a, src)
                eng = engs.next()
                if eng == 'act':
                    S.op('act', lambda e, a=a, b=b, n=g * Fw: e.activation(out=b[:, 0:n], in_=a[:, 0:n], func=AF.Copy), [a], [b])
                else:
                    S.op(eng, lambda e, a=a, b=b, n=g * Fw: e.tensor_copy(out=b[:, 0:n], in_=a[:, 0:n]), [a], [b])
                S.dma(dst[b0:b0 + g, :, :].rearrange("g p f -> p g f"), bv, dst, b, q='pool')
                nem += 1
                if nem % 8 == 0:
                    S.emit()
        P.close()

    def rmsnorm_fm(self, P, T, x, nch, gname, out_bf, ntok, goff=0):
        S = self.S
        K = T['K']
        NHh = ntok // 512
        sums = [T['psn'].next() for _ in range(NHh)]
        for kc in range(nch):
            sq = T['sq'].next()
            S.op('act', lambda e, sq=sq, kc=kc: e.activation(out=sq[:, 0:ntok], in_=x[:, kc, :], func=AF.Square), [x], [sq])
            for h in range(NHh):
                S.op('pe', lambda e, sq=sq, h=h, kc=kc, ps=sums[h]: e.matmul(ps[:], K['ones'][:], sq[:, h * 512:(h + 1) * 512],
                                                                            start=(kc == 0), stop=(kc == nch - 1)),
                     [sq, K['ones']], [sums[h]])
        rstd, tmp = T['rstd'], T['tmp']
        for h in range(NHh):
            S.op('dve', lambda e, h=h, ps=sums[h]: e.tensor_scalar(out=tmp[:, h * 512:(h + 1) * 512], in0=ps[:], scalar1=1.0 / (nch * 128),
                                                                   scalar2=EPS, op0=ALU.mult, op1=ALU.add), [sums[h]], [tmp])
        S.op('act', lambda e: e.activation(out=tmp[:, 0:ntok], in_=tmp[:, 0:ntok], func=AF.Sqrt), [tmp], [tmp])
        S.op('dve', lambda e: e.reciprocal(out=rstd[:, 0:ntok], in_=tmp[:, 0:ntok]), [tmp], [rstd])
        for kc in range(nch):
            S.op('dve', lambda e, kc=kc: e.scalar_tensor_tensor(out=out_bf[:, kc, :], in0=x[:, kc, :], scalar=self.pv(K, gname, goff + kc),
                                                                 in1=rstd[:, 0:ntok], op0=ALU.mult, op1=ALU.mult),
                 [x, rstd, K['pvec']], [out_bf])

    def wstream(self, T, wd, blk0, nblk, KC, kc0=0):
        S = self.S
        G = 2
        groups = {}

        def ensure(i):
            gi = i // G
            if gi in groups or gi * G >= nblk:
                return
            g = min(G, nblk - gi * G)
            wt = T['wring'].next()
            dv = wt[:, 0:g, 0:KC, :]
            sv = wd[blk0 + gi * G: blk0 + gi * G + g, :, kc0 * 128:(kc0 + KC) * 128].rearrange("g p (k m) -> p g k m", m=128)
            S.dma(dv, sv, wt, wd)
            groups[gi] = wt

        def get(i):
            ensure(i)
            return groups[i // G], i % G
        return ensure, get

    def mm_fm(self, T, wd, blk0, nmc, KC, rhs, ntok, evac, kc0=0, rhs_kc0=0):
        S = self.S
        ensure, get = self.wstream(T, wd, blk0, nmc, KC, kc0)
        NHh = ntok // 512
        for i in range(min(6, nmc)):
            ensure(i)
        for mc in range(nmc):
            ensure(mc + 6)
            wt, g = get(mc)
            for h in range(NHh):
                ps = T['psm'].next()
                for kc in range(KC):
                    S.op('pe', lambda e, wt=wt, g=g, kc=kc, h=h, ps=ps: e.matmul(ps[:], wt[:, g, kc, :], rhs[:, rhs_kc0 + kc, h * 512:(h + 1) * 512],
                                                                                start=(kc == 0), stop=(kc == KC - 1)),
                         [wt, rhs], [ps], inc=(kc == KC - 1))
                evac(mc, h, ps)

    def dense(self, lc, la):
        c, S = self.cfg, self.S
        D, DC, FC, TT, NH, FCQ = c.D, c.DC, c.FC, c.TT, c.NH, c.FCQ
        P = Phase(self, 'dn%s' % (la if la is not None else 'F'))
        T = {}
        T['K'] = K = self.load_consts(P)
        x = P.sb([128, DC, TT], F32, 'x')
        act = P.sb([128, max(16, DC), TT], BF16, 'act')
        hidn = max(FCQ * TT, 2 * DC * 512)
        hid = P.sb([128, hidn], BF16, 'hid')
        hidv = View(hid, hid[:, 0:FCQ * TT].rearrange("p (f t) -> p f t", f=FCQ))
        T['wring'] = Ring([P.sb([128, 2, 16, 128], BF16, 'w') for _ in range(4)])
        T['rstd'] = P.sb([128, TT], F32, 'rstd')
        T['tmp'] = P.sb([128, TT], F32, 'tmp')
        T['sq'] = Ring([P.sb([128, TT], BF16, 'sq') for _ in range(2)])
        stage = Ring([P.sb([128, TT], BF16, 'stage') for _ in range(4)])
        rl = Ring([P.sb([128, 512], BF16, 'rl') for _ in range(3)])
        T['psm'] = Ring([P.ps([128, 512], F32, 'psm') for _ in range(4)])
        T['psn'] = Ring([P.ps([128, 512], F32, 'psn') for _ in range(2)])
        pstr = Ring([P.ps([128, 512], F32, 'pst') for _ in range(2)])
        if lc is None or la is None:
            xio = Ring([P.sb([128, D], F32, 'xio') for _ in range(2)])

        for ti in range(c.NTILES):
            t0 = ti * TT
            xreg = self.xT.reg(ti)
            if lc is None:
                for tb in range(TT // 128):
                    xi = xio.next()
                    S.dma(xi[:], self.x_tok[t0 + tb * 128:t0 + (tb + 1) * 128, :], xi, self.x_tok)
                    for k4 in range(0, DC, 4):
                        ps = pstr.next()
                        n4 = min(4, DC - k4)
                        for j in range(n4):
                            S.op('pe', lambda e, xi=xi, ps=ps, j=j, kc=k4 + j: e.transpose(out=ps[:, j * 128:(j + 1) * 128], in_=xi[:, kc * 128:(kc + 1) * 128],
                                                                                          identity=K['identf'][:]),
                                 [xi, K['identf']], [ps], inc=(j == n4 - 1))
                        S.op('act', lambda e, ps=ps, k4=k4, n4=n4, tb=tb: e.activation(
                            out=x[:, k4:k4 + n4, tb * 128:(tb + 1) * 128], in_=ps[:, 0:n4 * 128].rearrange("p (k t) -> p k t", k=n4), func=AF.Copy),
                            [ps], [x])
            else:
                S.dma(x[:], self.xT[:, t0:t0 + TT].rearrange("(k p) t -> p k t", p=128), x, xreg)
                S.dma(act[:, 0:16, :], self.ymixT[:, t0:t0 + TT].rearrange("(k p) t -> p k t", p=128), act, self.ymixT)
                def ev_res(mc, h, ps):
                    S.op('dve', lambda e: e.tensor_tensor(out=x[:, mc, h * 512:(h + 1) * 512], in0=ps[:], in1=x[:, mc, h * 512:(h + 1) * 512], op=ALU.add),
                         [ps, x], [x])
                self.mm_fm(T, self.wb['out'], lc * DC, DC, 16, act, TT, ev_res)
                self.rmsnorm_fm(P, T, x, DC, 'g_mlp%d' % lc, act, TT)
                for hq in range(c.HQ):
                    def ev_up(mc, h, ps):
                        r = rl.next()
                        S.op('act', lambda e: e.activation(out=r[:], in_=ps[:], func=AF.Relu), [ps], [r])
                        S.op('pool', lambda e: e.tensor_tensor(out=hidv[:, mc, h * 512:(h + 1) * 512], in0=r[:], in1=r[:], op=ALU.mult), [r], [hid])
                    self.mm_fm(T, self.wb['up'], lc * FC + hq * FCQ, FCQ, DC, act, TT, ev_up)
                    self.mm_fm(T, self.wb['dn'], lc * DC, DC, FCQ, hidv, TT, ev_res, kc0=hq * FCQ)
                    S.emit()
            if la is None:
                self.final_out(P, T, x, t0, xio, pstr)
                S.emit()
                continue
            self.rmsnorm_fm(P, T, x, DC, 'g_mix%d' % la, act, TT)

            def ev_z(mc, h, ps, st={}):
                if h == 0:
                    st['t'] = stage.next()
                stg_ = st['t']
                eng = 'act' if (mc + h) % 2 == 0 else 'dve'
                if eng == 'act':
                    S.op('act', lambda e: e.activation(out=stg_[:, h * 512:(h + 1) * 512], in_=ps[:], func=AF.Copy), [ps], [stg_])
                else:
                    S.op('dve', lambda e: e.tensor_copy(out=stg_[:, h * 512:(h + 1) * 512], in_=ps[:]), [ps], [stg_])
                if h == NH - 1:
                    S.dma(self.zT[mc * 128:(mc + 1) * 128, t0:t0 + TT], stg_[:, 0:TT], self.zT, stg_, q='pool')
            self.mm_fm(T, self.wb['in_f'], la * 24, 24, DC, act, TT, ev_z)
            wtv = [hid[:, i * DC * 512:(i + 1) * DC * 512].rearrange("p (k n) -> p k n", k=DC) for i in range(2)]
            for ncn in range(3):
                wt = wtv[ncn % 2]
                S.dma(wt, self.wb['in_t'][la * 3 + ncn, :, :].rearrange("p (k n) -> p k n", k=DC), hid, self.wb['in_t'])
                for tb in range(TT // 128):
                    ps = T['psm'].next()
                    for kc in range(DC):
                        S.op('pe', lambda e, ps=ps, kc=kc, tb=tb, wt=wt: e.matmul(ps[:], act[:, kc, tb * 128:(tb + 1) * 128], wt[:, kc, :],
                                                                                 start=(kc == 0), stop=(kc == DC - 1)),
                             [act, hid], [ps], inc=(kc == DC - 1))
                    stg_ = stage.next()
                    if tb % 2 == 0:
                        S.op('act', lambda e, ps=ps, stg_=stg_: e.activation(out=stg_[:, 0:512], in_=ps[:], func=AF.Copy), [ps], [stg_])
                    else:
                        S.op('dve', lambda e, ps=ps, stg_=stg_: e.tensor_copy(out=stg_[:, 0:512], in_=ps[:]), [ps], [stg_])
                    r0 = t0 + tb * 128
                    if ncn < 2:
                        S.dma(self.vtok[r0:r0 + 128, ncn * 512:(ncn + 1) * 512], stg_[:, 0:512], self.vtok, stg_, q='pool')
                    else:
                        S.dma(self.xstok[r0:r0 + 128, :], stg_[:, 0:512], self.xstok, stg_, q='pool')
            S.dma(self.xT[:, t0:t0 + TT].rearrange("(k p) t -> p k t", p=128), x[:], (self.xT, xreg), x, q='pool')
            S.emit()
        P.close()

    def final_out(self, P, T, x, t0, xio, pstr):
        c, S = self.cfg, self.S
        K = T['K']
        DC, TT = c.DC, c.TT
        NHh = TT // 512
        sums = [T['psn'].next() for _ in range(NHh)]
        for kc in range(DC):
            sq = T['sq'].next()
            S.op('act', lambda e, sq=sq, kc=kc: e.activation(out=sq[:, 0:TT], in_=x[:, kc, :], func=AF.Square), [x], [sq])
            for h in range(NHh):
                S.op('pe', lambda e, sq=sq, h=h, kc=kc, ps=sums[h]: e.matmul(ps[:], K['ones'][:], sq[:, h * 512:(h + 1) * 512],
                                                                            start=(kc == 0), stop=(kc == DC - 1)),
                     [sq, K['ones']], [sums[h]])
        rstd, tmp = T['rstd'], T['tmp']
        for h in range(NHh):
            S.op('dve', lambda e, h=h, ps=sums[h]: e.tensor_scalar(out=tmp[:, h * 512:(h + 1) * 512], in0=ps[:], scalar1=1.0 / (DC * 128),
                                                                   scalar2=EPS, op0=ALU.mult, op1=ALU.add), [sums[h]], [tmp])
        S.op('act', lambda e: e.activation(out=tmp[:, 0:TT], in_=tmp[:, 0:TT], func=AF.Sqrt), [tmp], [tmp])
        S.op('dve', lambda e: e.reciprocal(out=rstd[:, 0:TT], in_=tmp[:, 0:TT]), [tmp], [rstd])
        for kc in range(DC):
            S.op('dve', lambda e, kc=kc: e.scalar_tensor_tensor(out=x[:, kc, :], in0=x[:, kc, :], scalar=self.pv(K, 'final_g', kc),
                                                                 in1=rstd[:, 0:TT], op0=ALU.mult, op1=ALU.mult),
                 [x, rstd, K['pvec']], [x])
        for tb in range(TT // 128):
            xo = xio.next()
            for k4 in range(0, DC, 4):
                ps = pstr.next()
                n4 = min(4, DC - k4)
                for j in range(n4):
                    S.op('pe', lambda e, ps=ps, j=j, kc=k4 + j, tb=tb: e.transpose(out=ps[:, j * 128:(j + 1) * 128], in_=x[:, kc, tb * 128:(tb + 1) * 128],
                                                                                 identity=K['identf'][:]),
                         [x, K['identf']], [ps], inc=(j == n4 - 1))
                S.op('act', lambda e, ps=ps, k4=k4, n4=n4, xo=xo: e.activation(out=xo[:, k4 * 128:(k4 + n4) * 128], in_=ps[:, 0:n4 * 128], func=AF.Copy),
                     [ps], [xo])
            S.dma(self.y_tok[t0 + tb * 128:t0 + (tb + 1) * 128, :], xo[:], self.y_tok, xo, q='pool')

    def prologue_tables(self):
        c, S = self.cfg, self.S
        P = Phase(self, 'ptab')
        cm = P.sb([128, 64], F32, 'nacm')
        S.dma(cm[:], self.nacm_d[:, :], cm, self.nacm_d)
        ring = Ring([P.sb([128, 15, 64], F32, 'nag') for _ in range(3)])
        for i in range(c.L * 8):
            t = ring.next()
            S.dma(t[:], self.nag_d[i, :, :].rearrange("p (a k) -> p a k", a=15), t, self.nag_d)
            S.op('dve', lambda e, t=t: e.tensor_tensor(out=t[:], in0=t[:], in1=mkap(cm[:], [[0, 15], [1, 64]]), op=ALU.add), [t, cm], [t])
            S.dma(self.natab[i, :, :].rearrange("p (a k) -> p a k", a=15), t[:], self.natab, t, q='pool')
        P.close()
        if hasattr(self, 's5_tables'):
            self.s5_tables()

    def ag1_layout(self):
        c = self.cfg
        off = 0
        lay = {}
        for si in range(2):
            for nm, n in (('kt', 1024 * 256), ('kb', 1024 * 256), ('vt', 192 * 1024), ('vb', 256 * 1024), ('xt', 512), ('xb', 1024)):
                lay[(si, nm)] = off
                off += n
        lay['tot'] = off
        return lay

    def mix_exchange1(self, l):
        c, S = self.cfg, self.S
        if not hasattr(self, 'ag1_in'):
            self.a1 = self.ag1_layout()
            tot = self.a1['tot']
            self.ag1_in = DT(self.nc.dram_tensor('ag1_in', [1, tot], BF16), 'ag1_in')
            self.ag1_out = DT(self.nc.dram_tensor('ag1_out', [NCORES, tot], BF16), 'ag1_out')
            self.hsel = self.dscr('hsel', [1, tot], BF16)
        a1 = self.a1
        P = Phase(self, 'ex1_%d' % l)
        zT, vt = self.zT, self.vtok
        for si, (T0, N, R) in enumerate(c.segs):
            def dst(nm, n, w):
                o = a1[(si, nm)]
                return self.ag1_in[0:1, o:o + n * w].rearrange("a (r w) -> (a r) w", w=w)
            S.dma(dst('kt', 1024, 256)[:, 0:192], zT[1024:2048, T0:T0 + 192], self.ag1_in, zT)
            S.dma(dst('kb', 1024, 256), zT[1024:2048, T0 + N - 256:T0 + N], self.ag1_in, zT)
            S.dma(dst('vt', 192, 1024), vt[T0:T0 + 192, :], self.ag1_in, vt)
            S.dma(dst('vb', 256, 1024), vt[T0 + N - 256:T0 + N, :], self.ag1_in, vt)
            S.dma(dst('xt', 512, 1), zT[2048:2560, T0:T0 + 1], self.ag1_in, zT, allow_slow_non_contiguous=True)
            S.dma(dst('xb', 512, 2), zT[2048:2560, T0 + N - 2:T0 + N], self.ag1_in, zT, allow_slow_non_contiguous=True)
        S.custom('pool', lambda e: e.collective_compute("AllGather", ALU.bypass, replica_groups=[list(range(NCORES))],
                                                        ins=[self.ag1_in.h.ap().opt()], outs=[self.ag1_out.h.ap().opt()]),
                 self.ag1_out, 1, reads=[self.ag1_in], writes=[self.ag1_out])
        cmask = P.sb([128, 260], F32, 'cmask')
        S.dma(cmask[:], self.cmask_d[:, :], cmask, self.cmask_d)
        CH = 2048
        ld = Ring([P.sb([128, CH], BF16, 'sl') for _ in range(4)])
        accs = Ring([P.sb([128, CH], BF16, 'acc') for _ in range(2)])
        sizes = {'kt': 1024 * 256, 'kb': 1024 * 256, 'vt': 192 * 1024, 'vb': 256 * 1024, 'xt': 512, 'xb': 1024}
        for si in range(2):
            for nm in ('kt', 'kb', 'vt', 'vb', 'xt', 'xb'):
                kind = 2 * si + (0 if nm[1] == 'b' else 1)
                o = a1[(si, nm)]
                ncol = sizes[nm] // 128
                for c0 in range(0, ncol, CH):
                    w = min(CH, ncol - c0)
                    acc = accs.next()
                    for r in range(NCORES):
                        t = ld.next()
                        S.dma(t[:, 0:w], self.ag1_out[r:r + 1, o:o + sizes[nm]].rearrange("a (p c) -> (a p) c", p=128)[:, c0:c0 + w], t, self.ag1_out)
                        msk = cmask[:, 228 + kind * 8 + r:228 + kind * 8 + r + 1]
                        eng = 'dve' if r % 2 == 0 else 'pool'
                        if r == 0:
                            S.op('dve', lambda e, t=t, acc=acc, w=w, msk=msk: e.tensor_scalar(out=acc[:, 0:w], in0=t[:, 0:w], scalar1=msk, scalar2=None, op0=ALU.mult),
                                 [t, cmask], [acc])
                        else:
                            S.op('dve', lambda e, t=t, acc=acc, w=w, msk=msk: e.scalar_tensor_tensor(out=acc[:, 0:w], in0=t[:, 0:w], scalar=msk, in1=acc[:, 0:w],
                                                                                                op0=ALU.mult, op1=ALU.add), [t, cmask, acc], [acc])
                    S.dma(self.hsel[0:1, o:o + sizes[nm]].rearrange("a (p c) -> (a p) c", p=128)[:, c0:c0 + w], acc[:, 0:w], self.hsel, acc, q='pool')
                S.emit()
        P.close()

    def mix_na(self, l):
        c, S = self.cfg, self.S
        P = Phase(self, 'na%d' % l)
        K = self.load_consts(P, want=('ones', 'identf', 'identb'))
        hs = self.hsel
        a1 = self.a1
        cmask = P.sb([128, 260], F32, 'cmask')
        S.dma(cmask[:], self.cmask_d[:, :], cmask, self.cmask_d)
        RM = max(c.RS, c.RP)
        NE = (RM + 8) * 64
        kTs = Ring([P.sb([128, NE], BF16, 'kT') for _ in range(2)])
        qTs = Ring([P.sb([128, RM * 64], BF16, 'qT') for _ in range(2)])
        Ves = Ring([P.sb([128, (RM + 8) // 2, 128], BF16, 'Ve') for _ in range(2)])
        Vos = Ring([P.sb([128, (RM + 8) // 2, 128], BF16, 'Vo') for _ in range(2)])
        BTs = Ring([P.sb([128, 15, 64], F32, 'BT') for _ in range(2)])
        Bedge = P.sb([128, 8, 768], F32, 'Bedge')
        yas = Ring([P.sb([128, RM * 64], F32, 'ya') for _ in range(2)])
        Tts = Ring([P.sb([128, 768], F32, 'T') for _ in range(2)])
        Pts = Ring([P.sb([128, 768], BF16, 'P') for _ in range(2)])
        PTs = Ring([P.sb([128, 768], BF16, 'PT') for _ in range(2)])
        rDs = Ring([P.sb([128, 64], F32, 'rD') for _ in range(2)])
        psS = Ring([P.ps([128, 512], F32, 'psS') for _ in range(4)])
        psT = Ring([P.ps([128, 1024], BF16, 'psT') for _ in range(2)])
        psO = Ring([P.ps([128, 512], F32, 'psO') for _ in range(2)])
        zT, vt = self.zT, self.vtok
        for si, (T0, N, R) in enumerate(c.segs):
            NEs = (R + 8) * 64
            import os
            for hp in range(int(os.environ.get('NA_HPS', '8'))):
                kT, qT, Ve, Vo, BT, ya = kTs.next(), qTs.next(), Ves.next(), Vos.next(), BTs.next(), yas.next()
                f0 = hp * 128
                S.dma(qT[:, 0:N], zT[f0:f0 + 128, T0:T0 + N], qT, zT)
                S.dma(kT[:, 256:256 + N], zT[1024 + f0:1024 + f0 + 128, T0:T0 + N], kT, zT)
                S.op('pool', lambda e, kT=kT, N=N: e.memset(kT[:, 256 + N + 192:256 + N + 256], 0.0), [], [kT])
                okb, okt = a1[(si, 'kb')], a1[(si, 'kt')]
                S.dma(kT[:, 0:256], hs[0:1, okb + f0 * 256:okb + (f0 + 128) * 256].rearrange("a (p t) -> (a p) t", t=256), kT, hs)
                S.dma(kT[:, 256 + N:256 + N + 192], hs[0:1, okt + f0 * 256:okt + (f0 + 128) * 256].rearrange("a (p t) -> (a p) t", t=256)[:, 0:192], kT, hs)
                nt2 = (R + 8) // 2
                S.op('pool', lambda e, Ve=Ve, Vo=Vo, nt2=nt2: e.memset(Ve[:, nt2 - 1, :], 0.0), [], [Ve])
                S.op('pool', lambda e, Vo=Vo, nt2=nt2: e.memset(Vo[:, nt2 - 2:nt2, :], 0.0), [], [Vo])
                ovb, ovt = a1[(si, 'vb')], a1[(si, 'vt')]

                def vsrc(o, tok0, ntok):
                    return hs[0:1, o + tok0 * 1024:o + (tok0 + ntok) * 1024].rearrange("a (t f) -> (a t) f", f=1024)[:, f0:f0 + 128]
                for tc in range(0, R // 2, 8):
                    nt_ = min(8, R // 2 - tc)
                    S.dma(Ve[:, 2 + tc:2 + tc + nt_, :], vt[T0 + tc * 128:T0 + (tc + nt_) * 128, f0:f0 + 128].rearrange("(t p) f -> p t f", p=128), Ve, vt)
                for t in range(2):
                    S.dma(Ve[:, t, :], vsrc(ovb, t * 128, 128), Ve, hs)
                S.dma(Ve[:, 2 + R // 2, :], vsrc(ovt, 0, 128), Ve, hs)
                S.dma(Ve[0:64, 3 + R // 2, :], vsrc(ovt, 128, 64), Ve, hs)
                S.dma(Vo[:, 0, :], vsrc(ovb, 64, 128), Vo, hs)
                S.dma(Vo[0:64, 1, :], vsrc(ovb, 192, 64), Vo, hs)
                S.dma(Vo[64:128, 1, :], vt[T0:T0 + 64, f0:f0 + 128], Vo, vt)
                for tc in range(0, R // 2 - 1, 8):
                    nt_ = min(8, R // 2 - 1 - tc)
                    S.dma(Vo[:, 2 + tc:2 + tc + nt_, :], vt[T0 + 64 + tc * 128:T0 + 64 + (tc + nt_) * 128, f0:f0 + 128].rearrange("(t p) f -> p t f", p=128), Vo, vt)
                S.dma(Vo[0:64, 1 + R // 2, :], vt[T0 + N - 64:T0 + N, f0:f0 + 128], Vo, vt)
                S.dma(Vo[64:128, 1 + R // 2, :], vsrc(ovt, 0, 64), Vo, hs)
                S.dma(Vo[:, 2 + R // 2, :], vsrc(ovt, 64, 128), Vo, hs)
                S.dma(BT[:], self.natab[l * 8 + hp, :, :].rearrange("p (a k) -> p a k", a=15), BT, self.natab)
                for eg in range(8):
                    rr = eg % 4
                    mo = 36 + (si * 8 + eg) * 12
                    S.op('pool', lambda e, BT=BT, eg=eg, rr=rr, mo=mo: e.tensor_tensor(
                        out=Bedge[:, eg, :].rearrange("p (a k) -> p a k", a=12), in0=BT[:, 3 - rr:15 - rr, :],
                        in1=mkap(cmask[:, mo:mo + 12], [[1, 12], [0, 64]]), op=ALU.add), [BT, cmask], [Bedge])
                for r in range(R):
                    if r < 4:
                        e0, NK, Bsl = 0, 768, Bedge[:, r, :]
                    elif r >= R - 4:
                        e0, NK, Bsl = R - 4, 768, Bedge[:, 4 + (r - (R - 4)), :]
                    else:
                        e0, NK, Bsl = r, 512, BT[:, 3:11, :].rearrange("p a k -> p (a k)")
                    nkb = NK // 128
                    pA = psS.next()
                    pB = psS.next() if NK == 768 else None
                    for a in range(2):
                        sl = slice(a * 64, (a + 1) * 64)
                        S.op('pe', lambda e, sl=sl, a=a, pA=pA, r=r, e0=e0, qT=qT, kT=kT: e.matmul(
                            pA[sl, :], qT[sl, r * 64:(r + 1) * 64], kT[sl, e0 * 64:e0 * 64 + 512], start=True, stop=True, tile_position=(a * 64, a * 64)),
                            [qT, kT], [pA], inc=(a == 1))
                        if pB is not None:
                            S.op('pe', lambda e, sl=sl, a=a, pB=pB, r=r, e0=e0, qT=qT, kT=kT: e.matmul(
                                pB[sl, 0:256], qT[sl, r * 64:(r + 1) * 64], kT[sl, e0 * 64 + 512:e0 * 64 + 768], start=True, stop=True,
                                tile_position=(a * 64, a * 64)), [qT, kT], [pB], inc=(a == 1))
                    Tt, Pt, PT, rD = Tts.next(), Pts.next(), PTs.next(), rDs.next()
                    S.op('dve', lambda e, pA=pA, Tt=Tt, Bsl=Bsl: e.scalar_tensor_tensor(out=Tt[:, 0:512], in0=pA[:], scalar=0.125, in1=Bsl[:, 0:512],
                                                                                      op0=ALU.mult, op1=ALU.add), [pA, BT, Bedge], [Tt])
                    if pB is not None:
                        S.op('dve', lambda e, pB=pB, Tt=Tt, Bsl=Bsl: e.scalar_tensor_tensor(out=Tt[:, 512:768], in0=pB[:, 0:256], scalar=0.125, in1=Bsl[:, 512:768],
                                                                                          op0=ALU.mult, op1=ALU.add), [pB, BT, Bedge], [Tt])
                    S.op('act', lambda e, Tt=Tt, Pt=Pt, NK=NK: e.activation(out=Pt[:, 0:NK], in_=Tt[:, 0:NK], func=AF.Exp), [Tt], [Pt])
                    pT = psT.next()
                    for j in range(nkb):
                        S.op('pe', lambda e, j=j, pT=pT, Pt=Pt: e.transpose(out=pT[:, j * 128:(j + 1) * 128], in_=Pt[:, j * 128:(j + 1) * 128], identity=K['identb'][:]),
                             [Pt, K['identb']], [pT], inc=(j == nkb - 1))
                    if r % 2 == 0:
                        S.op('act', lambda e, pT=pT, PT=PT, NK=NK: e.activation(out=PT[:, 0:NK], in_=pT[:, 0:NK], func=AF.Copy), [pT], [PT])
                    else:
                        S.op('pool', lambda e, pT=pT, PT=PT, NK=NK: e.tensor_copy(out=PT[:, 0:NK], in_=pT[:, 0:NK]), [pT], [PT]) if False else \
                            S.op('dve', lambda e, pT=pT, PT=PT, NK=NK: e.tensor_copy(out=PT[:, 0:NK], in_=pT[:, 0:NK]), [pT], [PT])
                    Vs = Ve if e0 % 2 == 0 else Vo
                    tv = e0 // 2
                    pO = psO.next()
                    for a in range(2):
                        sl = slice(a * 64, (a + 1) * 64)
                        for j in range(nkb):
                            S.op('pe', lambda e, sl=sl, a=a, j=j, pO=pO, Vs=Vs, tv=tv, PT=PT, nkb=nkb: e.matmul(
                                pO[sl, 0:64], Vs[:, tv + j, sl], PT[:, j * 128 + a * 64:j * 128 + (a + 1) * 64], start=(j == 0), stop=(j == nkb - 1),
                                tile_position=(0, a * 64)), [Vs, PT], [pO], inc=False)
                        for j in range(nkb):
                            S.op('pe', lambda e, sl=sl, a=a, j=j, pO=pO, PT=PT, nkb=nkb: e.matmul(
                                pO[sl, 64:128], K['ones'][:, 0:64], PT[:, j * 128 + a * 64:j * 128 + (a + 1) * 64], start=(j == 0), stop=(j == nkb - 1),
                                tile_position=(0, a * 64)), [K['ones'], PT], [pO], inc=(a == 1 and j == nkb - 1))
                    S.op('dve', lambda e, pO=pO, rD=rD: e.reciprocal(out=rD[:], in_=pO[:, 64:128]), [pO], [rD])
                    S.op('dve', lambda e, pO=pO, rD=rD, ya=ya, r=r: e.tensor_tensor(out=ya[:, r * 64:(r + 1) * 64], in0=pO[:, 0:64], in1=rD[:], op=ALU.mult),
                         [pO, rD], [ya])
                S.dma(self.ymixraw[f0:f0 + 128, T0:T0 + N], ya[:, 0:N], self.ymixraw, ya, q='pool')
                S.emit()
        P.close()

    def ag2_setup(self):
        if not hasattr(self, 'ag2_in'):
            self.ag2_in = DT(self.nc.dram_tensor('ag2_in', [1, 128 * 160], F32), 'ag2_in')
            self.ag2_out = DT(self.nc.dram_tensor('ag2_out', [NCORES, 128 * 160], F32), 'ag2_out')
            c = self.cfg
            self.lrus = self.dscr('lrus', [4, 512, c.NT], F32)

    def mix_exchange2(self, l):
        S = self.S
        P = Phase(self, 'ex2_%d' % l)
        S.custom('pool', lambda e: e.collective_compute("AllGather", ALU.bypass, replica_groups=[list(range(NCORES))],
                                                        ins=[self.ag2_in.h.ap().opt()], outs=[self.ag2_out.h.ap().opt()]),
                 self.ag2_out, 1, reads=[self.ag2_in], writes=[self.ag2_out])
        P.close()

    @staticmethod
    def lru_idx(c, si, d, k):
        return ((c * 2 + si) * 2 + d) * 2 + k

    def mix_lru_a(self, l):
        c, S = self.cfg, self.S
        self.ag2_setup()
        P = Phase(self, 'lruA%d' % l)
        K = self.load_consts(P, want=('pvec',))
        a1 = self.a1
        hs = self.hsel
        NTL = min(1024, c.NP)
        wbd = P.sb([128, 16, 128], BF16, 'wbd')
        S.dma(wbd[:], self.wb['lru'][l * 16:(l + 1) * 16, :, :].rearrange("g p m -> p g m"), wbd, self.wb['lru'])
        cp = P.sb([128, 8], F32, 'cp')
        cp2 = P.sb([128, 8], F32, 'cp2')
        o0, _ = self.lay['lam%d_0' % l]
        o1, _ = self.lay['lam%d_1' % l]
        for d, o in ((0, o0), (1, o1)):
            S.op('act', lambda e, d=d, o=o: e.activation(out=cp[:, d * 4:d * 4 + 4], in_=K['pvec'][:, o:o + 4], func=AF.Exp, scale=-1.0), [K['pvec']], [cp])
        S.op('act', lambda e: e.activation(out=cp[:], in_=cp[:], func=AF.Ln, bias=1.0), [cp], [cp])
        S.op('dve', lambda e: e.tensor_scalar(out=cp2[:], in0=cp[:], scalar1=-16.0, scalar2=None, op0=ALU.mult), [cp], [cp2])
        S.op('dve', lambda e: e.tensor_scalar(out=cp[:], in0=cp[:], scalar1=-8.0, scalar2=None, op0=ALU.mult), [cp], [cp])
        zeros = P.sb([128, NTL], F32, 'zeros')
        S.op('pool', lambda e: e.memset(zeros[:], 0.0), [], [zeros])
        pay = P.sb([128, 32], F32, 'pay')
        mk = lambda nm, dt, w=NTL: Ring([P.sb([128, w], dt, nm) for _ in range(2)])
        xrs, xcs, xbs, rs, is_, as_, ms, us, hs_, ps_ = (mk('xr', BF16, NTL + 4), mk('xc', F32), mk('xb', BF16), mk('r', F32), mk('i', F32),
                                                           mk('a', F32), mk('m', F32), mk('u', F32), mk('h', F32), mk('pc', F32))
        psg = Ring([P.ps([128, 512], F32, 'psg') for _ in range(4)])
        cw, _ = self.lay['conv_w%d' % l]
        cb, _ = self.lay['conv_b%d' % l]
        for si, (T0, N, R) in enumerate(c.segs):
            ntl = N // NTL
            for ch in range(4):
                frow = 2048 + ch * 128
                for d in range(2):
                    order = range(ntl) if d == 0 else range(ntl - 1, -1, -1)
                    prev_h = prev_p = None
                    for tl in order:
                        t0 = tl * NTL
                        xr, xc, xb, r_, i_, a_, m_, u_, h_, pc = (xrs.next(), xcs.next(), xbs.next(), rs.next(), is_.next(), as_.next(), ms.next(),
                                                                   us.next(), hs_.next(), ps_.next())
                        lo = max(t0 - 2, 0)
                        hi = min(t0 + NTL + 1, N)
                        S.dma(xr[:, lo - (t0 - 2):hi - (t0 - 2)], self.zT[frow:frow + 128, T0 + lo:T0 + hi], xr, self.zT)
                        if t0 == 0:
                            ob = a1[(si, 'xb')] + ch * 128 * 2
                            S.dma(xr[:, 0:2], hs[0:1, ob:ob + 256].rearrange("a (p j) -> (a p) j", j=2), xr, hs)
                        if t0 + NTL == N:
                            ot = a1[(si, 'xt')] + ch * 128
                            S.dma(xr[:, NTL + 2:NTL + 3], hs[0:1, ot:ot + 128].rearrange("a (p j) -> (a p) j", j=1), xr, hs)
                        S.op('dve', lambda e, xr=xr, xc=xc, ch=ch: e.tensor_scalar(out=xc[:], in0=xr[:, 0:NTL], scalar1=K['pvec'][:, cw + ch * 4:cw + ch * 4 + 1],
                                                                            scalar2=K['pvec'][:, cb + ch:cb + ch + 1], op0=ALU.mult, op1=ALU.add),
                             [xr, K['pvec']], [xc])
                        for j in range(1, 4):
                            S.op('dve', lambda e, xr=xr, xc=xc, j=j, ch=ch: e.scalar_tensor_tensor(out=xc[:], in0=xr[:, j:j + NTL],
                                                                                          scalar=K['pvec'][:, cw + ch * 4 + j:cw + ch * 4 + j + 1],
                                                                                          in1=xc[:], op0=ALU.mult, op1=ALU.add), [xr, xc, K['pvec']], [xc])
                        S.op('act', lambda e, xc=xc, xb=xb: e.activation(out=xb[:], in_=xc[:], func=AF.Copy), [xc], [xb])
                        for gi, (dst, bnm) in enumerate(((r_, 'b_a%d_%d' % (l, d)), (i_, 'b_x%d_%d' % (l, d)))):
                            bo, _ = self.lay[bnm]
                            for hh in range(NTL // 512):
                                pg = psg.next()
                                S.op('pe', lambda e, pg=pg, gi=gi, xb=xb, hh=hh, d=d, ch=ch: e.matmul(pg[:], wbd[:, (d * 2 + gi) * 4 + ch, :], xb[:, hh * 512:(hh + 1) * 512],
                                                                                        start=True, stop=True), [wbd, xb], [pg])
                                S.op('act', lambda e, pg=pg, dst=dst, hh=hh, bo=bo, ch=ch: e.activation(out=dst[:, hh * 512:(hh + 1) * 512], in_=pg[:], func=AF.Sigmoid,
                                                                                              bias=K['pvec'][:, bo + ch:bo + ch + 1]), [pg, K['pvec']], [dst])
                        S.op('act', lambda e, r_=r_, a_=a_, d=d, ch=ch: e.activation(out=a_[:], in_=r_[:], func=AF.Exp, scale=cp[:, d * 4 + ch:d * 4 + ch + 1]), [r_, cp], [a_])
                        S.op('act', lambda e, r_=r_, m_=m_, d=d, ch=ch: e.activation(out=m_[:], in_=r_[:], func=AF.Exp, scale=cp2[:, d * 4 + ch:d * 4 + ch + 1]), [r_, cp2], [m_])
                        S.op('act', lambda e, m_=m_: e.activation(out=m_[:], in_=m_[:], func=AF.Sqrt, scale=-1.0, bias=1.0), [m_], [m_])
                        S.op('pool', lambda e, i_=i_, xc=xc, u_=u_: e.tensor_tensor(out=u_[:], in0=i_[:], in1=xc[:], op=ALU.mult), [i_, xc], [u_])
                        S.op('pool', lambda e, m_=m_, u_=u_: e.tensor_tensor(out=u_[:], in0=u_[:], in1=m_[:], op=ALU.mult), [m_, u_], [u_])
                        if d == 0:
                            V = lambda t: t[:]
                            last = lambda t: t[:, NTL - 1:NTL]
                        else:
                            V = lambda t: mkap(t[:], [[-1, NTL]], off=NTL - 1)
                            last = lambda t: t[:, 0:1]
                        ini_h = 0.0 if prev_h is None else last(prev_h)
                        ini_p = 1.0 if prev_p is None else last(prev_p)
                        rd_h = [a_, u_] + ([prev_h] if prev_h is not None else [])
                        rd_p = [a_, zeros] + ([prev_p] if prev_p is not None else [])
                        S.op('dve', lambda e, a_=a_, u_=u_, h_=h_, ini_h=ini_h, V=V: e.tensor_tensor_scan(out=V(h_), data0=V(a_), data1=V(u_), initial=ini_h,
                                                                                                    op0=ALU.mult, op1=ALU.add), rd_h, [h_])
                        S.op('dve', lambda e, a_=a_, pc=pc, ini_p=ini_p, V=V: e.tensor_tensor_scan(out=V(pc), data0=V(a_), data1=V(zeros), initial=ini_p,
                                                                                             op0=ALU.mult, op1=ALU.add), rd_p, [pc])
                        S.dma(self.lrus[d * 2 + 0, ch * 128:(ch + 1) * 128, T0 + t0:T0 + t0 + NTL], h_[:], self.lrus, h_, q='pool')
                        S.dma(self.lrus[d * 2 + 1, ch * 128:(ch + 1) * 128, T0 + t0:T0 + t0 + NTL], pc[:], self.lrus, pc, q='pool')
                        prev_h, prev_p = h_, pc
                    ix = self.lru_idx(ch, si, d, 0)
                    S.op('pool', lambda e, pp=prev_p, ix=ix, last=last: e.tensor_copy(out=pay[:, ix:ix + 1], in_=last(pp)), [prev_p], [pay])
                    S.op('pool', lambda e, ph=prev_h, ix=ix, last=last: e.tensor_copy(out=pay[:, ix + 1:ix + 2], in_=last(ph)), [prev_h], [pay])
                    S.emit()
        S.dma(self.ag2_in[0:1, :].rearrange("a (p c) -> (a p) c", p=128)[:, 0:32], pay[:], self.ag2_in, pay, q='pool')
        P.close()

    def horner(self, P, cmask, seg, d, upd):
        order = range(NCORES) if d == 0 else range(NCORES - 1, -1, -1)
        for r in order:
            mo = 4 + (seg * 2 + d) * 8 + r
            upd(r, cmask[:, mo:mo + 1])

    def mix_lru_b(self, l):
        c, S = self.cfg, self.S
        P = Phase(self, 'lruB%d' % l)
        cmask = P.sb([128, 260], F32, 'cmask')
        S.dma(cmask[:], self.cmask_d[:, :], cmask, self.cmask_d)
        AG = P.sb([128, NCORES, 160], F32, 'ag')
        S.dma(AG[:], self.ag2_out[:, :].rearrange("r (p c) -> p r c", p=128), AG, self.ag2_out)
        hin = P.sb([128, 2, 2, 4], F32, 'hin')
        t1 = P.sb([128, 4], F32, 't1')
        S.op('dve', lambda e: e.memset(hin[:], 0.0), [], [hin])
        for si in range(2):
            for d in range(2):
                hv = hin[:, si, d, :]

                def upd(r, m, si=si, d=d, hv=hv):
                    i0 = self.lru_idx(0, si, d, 0)
                    Pr = mkap(AG[:, r, i0:i0 + 1], [[8, 4]])
                    Hr = mkap(AG[:, r, i0 + 1:i0 + 2], [[8, 4]])
                    S.op('dve', lambda e: e.tensor_tensor(out=t1[:], in0=Pr, in1=hv, op=ALU.mult), [AG, hin], [t1])
                    S.op('dve', lambda e: e.tensor_tensor(out=t1[:], in0=t1[:], in1=Hr, op=ALU.add), [AG, t1], [t1])
                    S.op('dve', lambda e: e.tensor_tensor(out=t1[:], in0=t1[:], in1=hv, op=ALU.subtract), [hin, t1], [t1])
                    S.op('dve', lambda e: e.scalar_tensor_tensor(out=hv, in0=t1[:], scalar=m, in1=hv, op0=ALU.mult, op1=ALU.add), [t1, hin, cmask], [hin])
                self.horner(P, cmask, si, d, upd)
        NTL = min(1024, c.NP)
        mk = lambda nm, dt: Ring([P.sb([128, NTL], dt, nm) for _ in range(2)])
        hf, pf, hb, pb, gt, gg, yo = mk('hf', F32), mk('pf', F32), mk('hb', F32), mk('pb', F32), mk('gt', BF16), mk('gg', F32), mk('yo', F32)
        for si, (T0, N, R) in enumerate(c.segs):
            for ch in range(4):
                for tl in range(N // NTL):
                    t0 = T0 + tl * NTL
                    a, b, c_, d_, g, g2, y = hf.next(), pf.next(), hb.next(), pb.next(), gt.next(), gg.next(), yo.next()
                    for k, tile_ in enumerate((a, b, c_, d_)):
                        S.dma(tile_[:], self.lrus[k, ch * 128:(ch + 1) * 128, t0:t0 + NTL], tile_, self.lrus)
                    S.dma(g[:], self.zT[2560 + ch * 128:2560 + (ch + 1) * 128, t0:t0 + NTL], g, self.zT)
                    S.op('dve', lambda e, a=a, b=b, si=si, ch=ch: e.scalar_tensor_tensor(out=a[:], in0=b[:], scalar=hin[:, si, 0, ch:ch + 1], in1=a[:], op0=ALU.mult, op1=ALU.add),
                         [a, b, hin], [a])
                    S.op('pool', lambda e, c_=c_, d_=d_, si=si, ch=ch: e.scalar_tensor_tensor(out=c_[:], in0=d_[:], scalar=hin[:, si, 1, ch:ch + 1], in1=c_[:], op0=ALU.mult, op1=ALU.add),
                         [c_, d_, hin], [c_]) if False else \
                        S.op('dve', lambda e, c_=c_, d_=d_, si=si, ch=ch: e.scalar_tensor_tensor(out=c_[:], in0=d_[:], scalar=hin[:, si, 1, ch:ch + 1], in1=c_[:], op0=ALU.mult, op1=ALU.add),
                             [c_, d_, hin], [c_])
                    S.op('pool', lambda e, a=a, c_=c_: e.tensor_tensor(out=a[:], in0=a[:], in1=c_[:], op=ALU.add), [a, c_], [a])
                    S.op('act', lambda e, g=g, g2=g2: e.activation(out=g2[:], in_=g[:], func=AF.Gelu_apprx_tanh), [g], [g2])
                    S.op('pool', lambda e, a=a, g2=g2, y=y: e.tensor_tensor(out=y[:], in0=a[:], in1=g2[:], op=ALU.mult), [a, g2], [y])
                    S.dma(self.ymixraw[1024 + ch * 128:1024 + (ch + 1) * 128, t0:t0 + NTL], y[:], self.ymixraw, y, q='pool')
                S.emit()
        P.close()

    def s5_tables(self):
        c, S = self.cfg, self.S
        L = c.L
        self.s5wbt = self.dscr('s5wbt', [L, 2, 128, 32 * 2 * 128], BF16)
        self.s5t0 = self.dscr('s5t0', [L, 128, 32 * 128], BF16)
        self.s5wc = self.dscr('s5wc', [L, 4, 128, 32 * 128], BF16)
        self.s5pw = self.dscr('s5pw', [L, 2, 128, 32 * S5E_N], F32)
        for l in range(L):
            P = Phase(self, 's5t%d' % l)
            K = self.load_consts(P, want=('identb',))
            prm = P.sb([128, 2144], F32, 'prm')
            S.dma(prm[:], self.s5p_d[l, :, :], prm, self.s5p_d)
            E = P.sb([128, S5E_N], F32, 'E')
            S.dma(E[:], self.s5e_d[:, :], E, self.s5e_d)
            cm = P.sb([128, 2, 128], F32, 'cm')
            S.dma(cm[:], self.s5cm_d[:, :, :], cm, self.s5cm_d)
            dmk = P.sb([128, 2], F32, 'dmk')
            S.dma(dmk[:], self.dirmask_d[:, :], dmk, self.dirmask_d)
            NE = 32 * S5E_N
            sm = lambda nm, w=32: P.sb([128, w], F32, nm)
            dt, ar, ai = sm('dt'), sm('ar'), sm('ai')
            import math as _m

            def exp_acc(dst, src_ap, tmp, rd):
                S.op('dve', lambda e: e.tensor_scalar(out=tmp[:], in0=src_ap, scalar1=1.0 / 16, scalar2=None, op0=ALU.mult), rd, [tmp])
                S.op('dve', lambda e: e.tensor_scalar(out=dst[:], in0=tmp[:], scalar1=1.0 / _m.factorial(10), scalar2=None, op0=ALU.mult), [tmp], [dst])
                for k in range(9, 0, -1):
                    S.op('dve', lambda e, k=k: e.scalar_tensor_tensor(out=dst[:], in0=dst[:], scalar=1.0 / _m.factorial(k), in1=tmp[:], op0=ALU.add, op1=ALU.mult),
                         [dst, tmp], [dst])
                S.op('dve', lambda e: e.tensor_scalar(out=dst[:], in0=dst[:], scalar1=1.0, scalar2=None, op0=ALU.add), [dst], [dst])
                for _ in range(4):
                    S.op('dve', lambda e: e.tensor_tensor(out=dst[:], in0=dst[:], in1=dst[:], op=ALU.mult), [dst], [dst])
            dtmp = sm('dtmp')
            exp_acc(dt, prm[:, 64:96], dtmp, [prm])
            S.op('dve', lambda e: e.tensor_tensor(out=ar[:], in0=prm[:, 0:32], in1=dt[:], op=ALU.mult), [prm, dt], [ar])
            S.op('dve', lambda e: e.tensor_tensor(out=ai[:], in0=prm[:, 32:64], in1=dt[:], op=ALU.mult), [prm, dt], [ai])
            X, PH, T1, T2 = sm('X', NE), sm('PH', NE), sm('T1', NE), sm('T2', NE)
            PWr, PWi = sm('PWr', NE), sm('PWi', NE)
            v3 = lambda t: t[:].rearrange("p (g e) -> p g e", g=32)
            gb = lambda t: mkap(t[:], [[1, 32], [0, S5E_N]])
            eb = mkap(E[:], [[0, 32], [1, S5E_N]])
            S.op('dve', lambda e: e.tensor_tensor(out=v3(X), in0=gb(ar), in1=eb, op=ALU.mult), [ar, E], [X])
            exp_acc(T1, X[:], T2, [X])
            S.op('dve', lambda e: e.tensor_copy(out=X[:], in_=T1[:]), [T1], [X])
            S.op('dve', lambda e: e.tensor_tensor(out=v3(PH), in0=gb(ai), in1=eb, op=ALU.mult), [ai, E], [PH])

            S.op('dve', lambda e: e.tensor_scalar(out=T2[:], in0=PH[:], scalar1=1.0 / TWO_PI, scalar2=MAGIC, op0=ALU.mult, op1=ALU.add), [PH], [T2])
            S.op('dve', lambda e: e.tensor_scalar(out=T2[:], in0=T2[:], scalar1=-MAGIC, scalar2=None, op0=ALU.add), [T2], [T2])
            S.op('dve', lambda e: e.scalar_tensor_tensor(out=T1[:], in0=T2[:], scalar=-TWO_PI, in1=PH[:], op0=ALU.mult, op1=ALU.add), [PH, T2], [T1])
            S.op('dve', lambda e: e.tensor_scalar(out=T1[:], in0=T1[:], scalar1=0.25, scalar2=None, op0=ALU.mult), [T1], [T1])
            S.op('dve', lambda e: e.tensor_tensor(out=T2[:], in0=T1[:], in1=T1[:], op=ALU.mult), [T1], [T2])
            sc = [-1.0 / 6, 1.0 / 120, -1.0 / 5040, 1.0 / 362880]
            cc = [-0.5, 1.0 / 24, -1.0 / 720, 1.0 / 40320]
            for dst, co_ in ((PWi, sc), (PWr, cc)):
                S.op('dve', lambda e, dst=dst, co_=co_: e.tensor_scalar(out=dst[:], in0=T2[:], scalar1=co_[3], scalar2=None, op0=ALU.mult), [T2], [dst])
                for k in (2, 1, 0):
                    S.op('dve', lambda e, dst=dst, co_=co_, k=k: e.scalar_tensor_tensor(out=dst[:], in0=dst[:], scalar=co_[k], in1=T2[:], op0=ALU.add, op1=ALU.mult),
                         [dst, T2], [dst])
            S.op('dve', lambda e: e.scalar_tensor_tensor(out=PWi[:], in0=PWi[:], scalar=1.0, in1=T1[:], op0=ALU.add, op1=ALU.mult), [PWi, T1], [PWi])
            S.op('dve', lambda e: e.tensor_scalar(out=PWr[:], in0=PWr[:], scalar1=1.0, scalar2=None, op0=ALU.add), [PWr], [PWr])
            for _ in range(2):
                S.op('dve', lambda e: e.tensor_tensor(out=T2[:], in0=PWi[:], in1=PWi[:], op=ALU.mult), [PWi], [T2])
                S.op('dve', lambda e: e.scalar_tensor_tensor(out=PWi[:], in0=PWi[:], scalar=2.0, in1=PWr[:], op0=ALU.mult, op1=ALU.mult), [PWi, PWr], [PWi])
                S.op('dve', lambda e: e.tensor_scalar(out=PWr[:], in0=T2[:], scalar1=-2.0, scalar2=1.0, op0=ALU.mult, op1=ALU.add), [T2], [PWr])
            S.op('dve', lambda e: e.tensor_tensor(out=PWr[:], in0=PWr[:], in1=X[:], op=ALU.mult), [PWr, X], [PWr])
            S.op('dve', lambda e: e.tensor_tensor(out=PWi[:], in0=PWi[:], in1=X[:], op=ALU.mult), [PWi, X], [PWi])
            colv = lambda t, k: mkap(t[:, k:k + 1], [[S5E_N, 32]])
            sq_r, sq_i, sq_t = sm('sq_r'), sm('sq_i'), sm('sq_t')

            def cpow2(src, dstc, nsq):
                S.op('dve', lambda e: e.tensor_copy(out=sq_r[:], in_=colv(PWr, src)), [PWr], [sq_r])
                S.op('dve', lambda e: e.tensor_copy(out=sq_i[:], in_=colv(PWi, src)), [PWi], [sq_i])
                for _ in range(nsq):
                    S.op('dve', lambda e: e.tensor_tensor(out=sq_t[:], in0=sq_r[:], in1=sq_i[:], op=ALU.mult), [sq_r, sq_i], [sq_t])
                    S.op('dve', lambda e: e.tensor_tensor(out=sq_r[:], in0=sq_r[:], in1=sq_r[:], op=ALU.mult), [sq_r], [sq_r])
                    S.op('dve', lambda e: e.tensor_tensor(out=sq_i[:], in0=sq_i[:], in1=sq_i[:], op=ALU.mult), [sq_i], [sq_i])
                    S.op('dve', lambda e: e.tensor_tensor(out=sq_r[:], in0=sq_r[:], in1=sq_i[:], op=ALU.subtract), [sq_r, sq_i], [sq_r])
                    S.op('dve', lambda e: e.tensor_scalar(out=sq_i[:], in0=sq_t[:], scalar1=2.0, scalar2=None, op0=ALU.mult), [sq_t], [sq_i])
                S.op('dve', lambda e: e.tensor_copy(out=colv(PWr, dstc), in_=sq_r[:]), [sq_r], [PWr])
                S.op('dve', lambda e: e.tensor_copy(out=colv(PWi, dstc), in_=sq_i[:]), [sq_i], [PWi])
            import math
            cpow2(25, 26, 5)
            cpow2(26, 27, int(round(math.log2(c.NS // 256))))
            cpow2(26, 28, int(round(math.log2(c.NP // 256))))
            S.dma(self.s5pw[l, 0, :, :], PWr[:], self.s5pw, PWr, q='pool')
            S.dma(self.s5pw[l, 1, :, :], PWi[:], self.s5pw, PWi, q='pool')
            pw = lambda t, e0, n=1: mkap(t[:, e0:e0 + 1], [[S5E_N, 32]]) if n == 1 else None
            nre, den, core, coim, t32 = sm('nre'), sm('den'), sm('core'), sm('coim'), sm('t32')
            L1r, L1i = pw(PWr, 24), pw(PWi, 24)
            are, aim = prm[:, 0:32], prm[:, 32:64]
            S.op('dve', lambda e: e.tensor_scalar(out=nre[:], in0=L1r, scalar1=-1.0, scalar2=None, op0=ALU.add), [PWr], [nre])
            S.op('dve', lambda e: e.tensor_tensor(out=den[:], in0=are, in1=are, op=ALU.mult), [prm], [den])
            S.op('dve', lambda e: e.tensor_tensor(out=t32[:], in0=aim, in1=aim, op=ALU.mult), [prm], [t32])
            S.op('dve', lambda e: e.tensor_tensor(out=den[:], in0=den[:], in1=t32[:], op=ALU.add), [den, t32], [den])
            S.op('dve', lambda e: e.reciprocal(out=den[:], in_=den[:]), [den], [den])
            S.op('dve', lambda e: e.tensor_tensor(out=core[:], in0=nre[:], in1=are, op=ALU.mult), [nre, prm], [core])
            S.op('dve', lambda e: e.tensor_tensor(out=t32[:], in0=L1i, in1=aim, op=ALU.mult), [PWi, prm], [t32])
            S.op('dve', lambda e: e.tensor_tensor(out=core[:], in0=core[:], in1=t32[:], op=ALU.add), [core, t32], [core])
            S.op('dve', lambda e: e.tensor_tensor(out=core[:], in0=core[:], in1=den[:], op=ALU.mult), [core, den], [core])
            S.op('dve', lambda e: e.tensor_tensor(out=coim[:], in0=L1i, in1=are, op=ALU.mult), [PWi, prm], [coim])
            S.op('dve', lambda e: e.tensor_tensor(out=t32[:], in0=nre[:], in1=aim, op=ALU.mult), [nre, prm], [t32])
            S.op('dve', lambda e: e.tensor_tensor(out=coim[:], in0=coim[:], in1=t32[:], op=ALU.subtract), [coim, t32], [coim])
            S.op('dve', lambda e: e.tensor_tensor(out=coim[:], in0=coim[:], in1=den[:], op=ALU.mult), [coim, den], [coim])
            bbr, bbi, t512 = sm('bbr', 512), sm('bbi', 512), sm('t512', 512)
            g16 = lambda t: mkap(t[:], [[1, 32], [0, 16]])
            v16 = lambda ap: ap.rearrange("p (g h) -> p g h", g=32)
            bre, bim = prm[:, 96:608], prm[:, 608:1120]
            cre, cim = prm[:, 1120:1632], prm[:, 1632:2144]

            def cmul_bc(outr, outi, ar_, ai_, br_, bi_, tmp, rd, wr):
                S.op('dve', lambda e: e.tensor_tensor(out=outr, in0=ar_, in1=br_, op=ALU.mult), rd, wr)
                S.op('dve', lambda e: e.tensor_tensor(out=tmp, in0=ai_, in1=bi_, op=ALU.mult), rd, wr)
                S.op('dve', lambda e: e.tensor_tensor(out=outr, in0=outr, in1=tmp, op=ALU.subtract), rd, wr)
                S.op('dve', lambda e: e.tensor_tensor(out=outi, in0=ar_, in1=bi_, op=ALU.mult), rd, wr)
                S.op('dve', lambda e: e.tensor_tensor(out=tmp, in0=ai_, in1=br_, op=ALU.mult), rd, wr)
                S.op('dve', lambda e: e.tensor_tensor(out=outi, in0=outi, in1=tmp, op=ALU.add), rd, wr)
            cmul_bc(v16(bbr[:]), v16(bbi[:]), g16(core), g16(coim), v16(bre), v16(bim), v16(t512[:]), [core, coim, prm, bbr, bbi, t512], [bbr, bbi, t512])
            F1, F2, F3 = sm('F1', 4096), sm('F2', 4096), sm('F3', 4096)
            v4 = lambda t: t[:].rearrange("p (g s h) -> p g s h", g=32, s=8)
            pw8 = lambda t, e0: mkap(t[:, e0:e0 + 1], [[S5E_N, 32], [1, 8], [0, 16]])
            x16 = lambda ap: mkap(ap, [[16, 32], [0, 8], [1, 16]])
            Abf = [P.sb([128, 32, 128], BF16, 'Abf%d' % i) for i in range(2)]
            Bmk = [P.sb([128, 32, 128], BF16, 'Bmk%d' % i) for i in range(4)]
            ring = Ring([P.sb([128, 32, 128], BF16, 'wcst') for _ in range(2)])
            allF = [F1, F2, F3, PWr, PWi, bbr, bbi, prm]
            cmul_bc(v4(F1), v4(F2), pw8(PWr, 0), pw8(PWi, 0), x16(bbr[:]), x16(bbi[:]), v4(F3), allF, [F1, F2, F3])
            S.op('act', lambda e: e.activation(out=Abf[0][:].rearrange("p g m -> p (g m)"), in_=F1[:], func=AF.Copy), [F1], [Abf[0]])
            S.op('act', lambda e: e.activation(out=Abf[1][:].rearrange("p g m -> p (g m)"), in_=F2[:], func=AF.Copy), [F2], [Abf[1]])
            cmul_bc(v4(F1), v4(F2), pw8(PWr, 16), pw8(PWi, 16), x16(cre), x16(cim), v4(F3), allF, [F1, F2, F3])
            for d in range(2):
                S.op('dve', lambda e, d=d: e.tensor_scalar(out=Bmk[2 * d][:].rearrange("p g m -> p (g m)"), in0=F1[:], scalar1=dmk[:, d:d + 1], scalar2=None,
                                                         op0=ALU.mult), [F1, dmk], [Bmk[2 * d]])
                S.op('dve', lambda e, d=d: e.tensor_scalar(out=Bmk[2 * d + 1][:].rearrange("p g m -> p (g m)"), in0=F2[:], scalar1=dmk[:, d:d + 1], scalar2=-1.0,
                                                         op0=ALU.mult, op1=ALU.mult), [F2, dmk], [Bmk[2 * d + 1]])
            cmul_bc(v4(F1), v4(F2), pw8(PWr, 8), pw8(PWi, 8), x16(cre), x16(cim), v4(F3), allF, [F1, F2, F3])
            for d in range(2):
                w0 = ring.next()
                S.op('dve', lambda e, d=d, w0=w0: e.tensor_scalar(out=w0[:].rearrange("p g m -> p (g m)"), in0=F1[:], scalar1=dmk[:, d:d + 1], scalar2=None,
                                                                op0=ALU.mult), [F1, dmk], [w0])
                S.dma(self.s5wc[l, 2 * d, :, :], w0[:].rearrange("p g m -> p (g m)"), self.s5wc, w0, q='pool')
                w1 = ring.next()
                S.op('dve', lambda e, d=d, w1=w1: e.tensor_scalar(out=w1[:].rearrange("p g m -> p (g m)"), in0=F2[:], scalar1=dmk[:, d:d + 1], scalar2=-1.0,
                                                                op0=ALU.mult, op1=ALU.mult), [F2, dmk], [w1])
                S.dma(self.s5wc[l, 2 * d + 1, :, :], w1[:].rearrange("p g m -> p (g m)"), self.s5wc, w1, q='pool')
            pst = Ring([P.ps([128, 1024], BF16, 'pst') for _ in range(2)])
            psm = Ring([P.ps([128, 512], F32, 'psm') for _ in range(4)])
            wbt = [P.sb([128, 32, 2, 128], BF16, 'wbt%d' % i) for i in range(2)]
            t0t = P.sb([128, 32, 128], BF16, 't0t')
            t0a = Ring([P.sb([128, 128], F32, 't0a') for _ in range(2)])
            for i in range(2):
                S.op('pool', lambda e, i=i: e.memset(wbt[i][:].rearrange("p g r m -> p (g r m)"), 0.0), [], [wbt[i]])
            for g in range(32):
                pt = pst.next()
                for ri in range(2):
                    S.op('pe', lambda e, g=g, ri=ri, pt=pt: e.transpose(out=pt[:, ri * 128:(ri + 1) * 128], in_=Abf[ri][:, g, :], identity=K['identb'][:]),
                         [Abf[ri], K['identb']], [pt], inc=(ri == 1))
                S.op('act', lambda e, g=g, pt=pt: e.activation(out=wbt[0][:, g, :, 0:64], in_=pt[:, 0:256].rearrange("p (r m) -> p r m", r=2)[:, :, 0:64],
                                                             func=AF.Copy), [pt], [wbt[0]])
                S.op('dve', lambda e, g=g, pt=pt: e.tensor_copy(out=wbt[1][:, g, :, 64:128], in_=pt[:, 0:256].rearrange("p (r m) -> p r m", r=2)[:, :, 64:128]),
                     [pt], [wbt[1]])
                ta = t0a.next()
                for d in range(2):
                    pm = psm.next()
                    S.op('pe', lambda e, g=g, d=d, pm=pm: e.matmul(pm[:, 0:128], Abf[0][:, g, :], Bmk[2 * d][:, g, :], start=True, stop=False),
                         [Abf[0], Bmk[2 * d]], [pm], inc=False)
                    S.op('pe', lambda e, g=g, d=d, pm=pm: e.matmul(pm[:, 0:128], Abf[1][:, g, :], Bmk[2 * d + 1][:, g, :], start=False, stop=True),
                         [Abf[1], Bmk[2 * d + 1]], [pm])
                    if d == 0:
                        S.op('dve', lambda e, pm=pm, ta=ta: e.tensor_tensor(out=ta[:], in0=pm[:, 0:128], in1=cm[:, 0, :], op=ALU.mult), [pm, cm], [ta])
                    else:
                        S.op('dve', lambda e, pm=pm, ta=ta: e.tensor_tensor(out=pm[:, 128:256], in0=pm[:, 0:128], in1=cm[:, 1, :], op=ALU.mult), [pm, cm], [pm])
                        S.op('dve', lambda e, pm=pm, ta=ta, g=g: e.tensor_tensor(out=t0t[:, g, :], in0=pm[:, 128:256], in1=ta[:], op=ALU.add), [pm, ta], [t0t])
                if g % 8 == 7:
                    S.emit()
            for i in range(2):
                S.dma(self.s5wbt[l, i, :, :], wbt[i][:].rearrange("p g r m -> p (g r m)"), self.s5wbt, wbt[i], q='pool')
            S.dma(self.s5t0[l, :, :], t0t[:].rearrange("p g m -> p (g m)"), self.s5t0, t0t, q='pool')
            P.close()

    def s5_setup(self):
        if hasattr(self, 's5u'):
            return
        c = self.cfg
        self.s5u = self.dscr('s5u', [128, 32 * c.NSC], BF16)
        self.s5send = self.dscr('s5send', [128, 2 * 32 * c.NBLK], F32)
        self.ystok = self.dscr('ystok', [c.NT, 512], F32)

    def s5_load_small(self, P, l):
        S = self.S
        pw = [P.sb([128, 32, S5E_N], F32, 'pw%d' % i) for i in range(2)]
        for i in range(2):
            S.dma(pw[i][:], self.s5pw[l, i, :, :].rearrange("p (g e) -> p g e", g=32), pw[i], self.s5pw)
        out = {}
        for nm, col in (('L8', 25), ('LJ', 26), ('LS', 27), ('LP', 28)):
            re = P.sb([128, 32], F32, nm + 're')
            sg = P.sb([128, 2, 32], F32, nm + 'sg')
            S.op('dve', lambda e, re=re, col=col: e.tensor_copy(out=re[:], in_=pw[0][:, :, col]), [pw[0]], [re])
            S.op('dve', lambda e, sg=sg, col=col: e.tensor_scalar(out=sg[:, 0, :], in0=pw[1][:, :, col], scalar1=-1.0, scalar2=None, op0=ALU.mult), [pw[1]], [sg])
            S.op('dve', lambda e, sg=sg, col=col: e.tensor_copy(out=sg[:, 1, :], in_=pw[1][:, :, col]), [pw[1]], [sg])
            out[nm] = (re, sg)
        return out

    def s5_cmul(self, eng, out_ap, st_ap, st_sw_ap, Lre_ap, Lsg_ap, t2_ap, rd, wr):
        S = self.S
        S.op(eng, lambda e: e.tensor_tensor(out=out_ap, in0=st_ap, in1=Lre_ap, op=ALU.mult), rd, wr)
        S.op(eng, lambda e: e.tensor_tensor(out=t2_ap, in0=st_sw_ap, in1=Lsg_ap, op=ALU.mult), rd, wr)
        S.op(eng, lambda e: e.tensor_tensor(out=out_ap, in0=out_ap, in1=t2_ap, op=ALU.add), rd, wr)

    def s5_bstage(self, P, T, l, g0, U, Bst):
        c, S = self.cfg, self.S
        wf, wbk = T['wbtf'], T['wbtb']
        k = 0
        for gl in range(8):
            g = g0 + gl
            for ri in range(2):
                for si, (T0, N, R) in enumerate(c.segs):
                    c0, ns = T0 // 8, N // 8
                    for cc in range(0, ns, 512):
                        w = min(512, ns - cc)
                        ps = T['psb'].next()
                        S.op('pe', lambda e, ps=ps, gl=gl, g=g, ri=ri, c0=c0, cc=cc, w=w: e.matmul(ps[:, 0:w], wf[:, gl, ri, :], U[:, g, c0 + cc:c0 + cc + w],
                                                                                              start=True, stop=False), [wf, U], [ps], inc=False)
                        S.op('pe', lambda e, ps=ps, gl=gl, g=g, ri=ri, c0=c0, cc=cc, w=w, ns=ns: e.matmul(
                            ps[:, 0:w], wbk[:, gl, ri, :], mkap(U[:, g, c0 + ns - 1 - cc:c0 + ns - cc], [[-1, w]]), start=False, stop=True), [wbk, U], [ps])
                        dst = Bst[:, ri, gl, c0 + cc:c0 + cc + w]
                        if k % 2 == 0:
                            S.op('act', lambda e, ps=ps, dst=dst, w=w: e.activation(out=dst, in_=ps[:, 0:w], func=AF.Copy), [ps], [Bst])
                        else:
                            S.op('dve', lambda e, ps=ps, dst=dst, w=w: e.tensor_copy(out=dst, in_=ps[:, 0:w]), [ps], [Bst])
                        k += 1

    def s5_load_batch_w(self, P, T, l, g0):
        S = self.S
        for nm, fb in (('wbtf', 0), ('wbtb', 1)):
            S.dma(T[nm][:].rearrange("p g r m -> p (g r m)"), self.s5wbt[l, fb, :, g0 * 256:(g0 + 8) * 256], T[nm], self.s5wbt)

    def mix_s5_a(self, l):
        c, S = self.cfg, self.S
        self.s5_setup()
        self.ag2_setup()
        P = Phase(self, 's5A%d' % l)
        K = self.load_consts(P, want=('identb',))
        NSC, NBLK, J = c.NSC, c.NBLK, c.J
        U = P.sb([128, 32, NSC], BF16, 'U')
        xs_ = Ring([P.sb([128, 8 * 512], BF16, 'xs') for _ in range(2)])
        xp_ = Ring([P.sb([128, 32, 128], BF16, 'xp') for _ in range(2)])
        pst = Ring([P.ps([128, 1024], BF16, 'pst') for _ in range(2)])
        for si, (T0, N, R) in enumerate(c.segs):
            for cb0 in range(0, N // 8, 128):
                nsc = min(128, N // 8 - cb0)
                tok0 = T0 + cb0 * 8
                xs, xp = xs_.next(), xp_.next()
                S.dma(xs[0:nsc, :], self.xstok[tok0:tok0 + nsc * 8, :].rearrange("(c s) f -> c (s f)", s=8), xs, self.xstok)
                S.op('pool', lambda e, xs=xs, xp=xp, nsc=nsc: e.tensor_copy(out=xp[0:nsc, :, :].rearrange("p g (s h) -> p g s h", s=8),
                                                                          in_=mkap(xs[0:nsc, 0:1], [[16, 32], [512, 8], [1, 16]])), [xs], [xp])
                for g4 in range(0, 32, 8):
                    pt = pst.next()
                    for j in range(8):
                        S.op('pe', lambda e, pt=pt, j=j, g=g4 + j, xp=xp, nsc=nsc: e.transpose(out=pt[:, j * 128:j * 128 + nsc], in_=xp[0:nsc, g, :],
                                                                                             identity=K['identb'][0:nsc, 0:nsc]), [xp, K['identb']], [pt], inc=(j == 7))
                    cglob = T0 // 8 + cb0
                    src = pt[:, :].rearrange("p (j c) -> p j c", j=8)[:, :, 0:nsc]
                    if (g4 // 8) % 2 == 0:
                        S.op('act', lambda e, pt=pt, g4=g4, cglob=cglob, nsc=nsc, src=src: e.activation(out=U[:, g4:g4 + 8, cglob:cglob + nsc], in_=src, func=AF.Copy),
                             [pt], [U])
                    else:
                        S.op('dve', lambda e, pt=pt, g4=g4, cglob=cglob, nsc=nsc, src=src: e.tensor_copy(out=U[:, g4:g4 + 8, cglob:cglob + nsc], in_=src), [pt], [U])
        S.dma(self.s5u[:, :], U[:].rearrange("p g c -> p (g c)"), self.s5u, U, q='pool')
        S.emit()
        Lc = self.s5_load_small(P, l)
        T = {'wbtf': P.sb([128, 8, 2, 128], BF16, 'wbtf'), 'wbtb': P.sb([128, 8, 2, 128], BF16, 'wbtb'),
             'psb': Ring([P.ps([128, 512], F32, 'psb') for _ in range(4)])}
        Bst = P.sb([128, 2, 8, NSC], F32, 'Bst')
        Send = P.sb([128, 2, 32, NBLK], F32, 'Send')
        t1 = P.sb([128, 2, 8, NBLK], F32, 't1')
        t2 = P.sb([128, 2, 8, NBLK], F32, 't2')
        S.op('dve', lambda e: e.memset(Send[:].rearrange("p a g b -> p (a g b)"), 0.0), [], [Send])
        L8re, L8sg = Lc['L8']
        for g0 in range(0, 32, 8):
            self.s5_load_batch_w(P, T, l, g0)
            self.s5_bstage(P, T, l, g0, U, Bst)
            St = Send[:, :, g0:g0 + 8, :]
            St_sw = mkap(Send[:, 1:2, g0:g0 + 1, 0:1], [[-32 * NBLK, 2], [NBLK, 8], [1, NBLK]])
            Lre = mkap(L8re[:, g0:g0 + 1], [[0, 2], [1, 8], [0, NBLK]])
            Lsg = mkap(L8sg[:, 0:1, g0:g0 + 1], [[32, 2], [1, 8], [0, NBLK]])
            for r in range(J):
                eng = 'dve'
                self.s5_cmul(eng, t1[:], St, St_sw, Lre, Lsg, t2[:], [Send, L8re, L8sg, t1, t2], [t1, t2])
                br = mkap(Bst[:, 0:1, 0:1, r:r + 1], [[8 * NSC, 2], [NSC, 8], [J, NBLK]])
                S.op(eng, lambda e, St=St, br=br: e.tensor_tensor(out=St, in0=t1[:], in1=br, op=ALU.add), [t1, Bst], [Send])
            S.emit()
        S.dma(self.s5send[:, :], Send[:].rearrange("p a g b -> p (a g b)"), self.s5send, Send, q='pool')
        LJre, LJsg = Lc['LJ']
        cs = P.sb([128, 2, 32], F32, 'cs')
        ct = P.sb([128, 2, 32], F32, 'ct')
        ct2 = P.sb([128, 2, 32], F32, 'ct2')
        pay = P.sb([128, 2, 32, 2], F32, 'pay')
        LJr3 = mkap(LJre[:, 0:1], [[0, 2], [1, 32]])
        for si, (b0, nb) in enumerate(c.segblk):
            S.op('dve', lambda e: e.memset(cs[:].rearrange("p a g -> p (a g)"), 0.0), [], [cs])
            for j in range(nb):
                cs_sw = mkap(cs[:, 1:2, 0:1], [[-32, 2], [1, 32]])
                self.s5_cmul('dve', ct[:], cs[:], cs_sw, LJr3, LJsg[:], ct2[:], [cs, LJre, LJsg, ct, ct2], [ct, ct2])
                S.op('dve', lambda e, j=j, b0=b0: e.tensor_tensor(out=cs[:], in0=ct[:], in1=Send[:, :, :, b0 + j], op=ALU.add), [ct, Send], [cs])
            S.op('dve', lambda e, si=si: e.tensor_copy(out=pay[:, :, :, si], in_=cs[:]), [cs], [pay])
        S.dma(self.ag2_in[0:1, :].rearrange("a (p c) -> (a p) c", p=128)[:, 32:160], pay[:].rearrange("p a g s -> p (a g s)"), self.ag2_in, pay, q='pool')
        P.close()

    def mix_s5_b(self, l):
        c, S = self.cfg, self.S
        P = Phase(self, 's5B%d' % l)
        K = self.load_consts(P, want=('identf',))
        NSC, NBLK, J = c.NSC, c.NBLK, c.J
        cmask = P.sb([128, 260], F32, 'cmask')
        S.dma(cmask[:], self.cmask_d[:, :], cmask, self.cmask_d)
        AG = P.sb([128, NCORES, 160], F32, 'ag')
        S.dma(AG[:], self.ag2_out[:, :].rearrange("r (p c) -> p r c", p=128), AG, self.ag2_out)
        Lc = self.s5_load_small(P, l)
        U = P.sb([128, 32, NSC], BF16, 'U')
        S.dma(U[:].rearrange("p g c -> p (g c)"), self.s5u[:, :], U, self.s5u)
        Send = P.sb([128, 2, 32, NBLK], F32, 'Send')
        S.dma(Send[:].rearrange("p a g b -> p (a g b)"), self.s5send[:, :], Send, self.s5send)
        cin0 = P.sb([128, 2, 32, 2], F32, 'cin0')
        hs_ = P.sb([128, 2, 32], F32, 'hs')
        ht = P.sb([128, 2, 32], F32, 'ht')
        ht2 = P.sb([128, 2, 32], F32, 'ht2')
        for si in range(2):
            Lre, Lsg = Lc['LS' if si == 0 else 'LP']
            S.op('dve', lambda e: e.memset(hs_[:].rearrange("p a g -> p (a g)"), 0.0), [], [hs_])
            for d in range(2):
                rows = slice(d * 64, d * 64 + 64)

                def upd(r, m, rows=rows, si=si, Lre=Lre, Lsg=Lsg):
                    Er = mkap(AG[rows, r, 32 + si:32 + si + 1], [[64, 2], [2, 32]])
                    hv = hs_[rows, :, :]
                    hsw = mkap(hs_[rows, 1:2, 0:1], [[-32, 2], [1, 32]])
                    self.s5_cmul('dve', ht[rows, :, :], hv, hsw, mkap(Lre[rows, 0:1], [[0, 2], [1, 32]]), Lsg[rows, :, :], ht2[rows, :, :],
                                 [hs_, Lre, Lsg, ht, ht2], [ht, ht2])
                    S.op('dve', lambda e: e.tensor_tensor(out=ht[rows, :, :], in0=ht[rows, :, :], in1=Er, op=ALU.add), [ht, AG], [ht])
                    S.op('dve', lambda e: e.tensor_tensor(out=ht[rows, :, :], in0=ht[rows, :, :], in1=hv, op=ALU.subtract), [ht, hs_], [ht])
                    S.op('dve', lambda e: e.scalar_tensor_tensor(out=hv, in0=ht[rows, :, :], scalar=m[rows, :], in1=hv, op0=ALU.mult, op1=ALU.add),
                         [ht, hs_, cmask], [hs_])
                self.horner(P, cmask, si, d, upd)
            S.op('dve', lambda e, si=si: e.tensor_copy(out=cin0[:, :, :, si], in_=hs_[:]), [hs_], [cin0])
        LJre, LJsg = Lc['LJ']
        Cin = P.sb([128, 2, 32, NBLK], F32, 'Cin')
        ct = P.sb([128, 2, 32], F32, 'ct')
        ct2 = P.sb([128, 2, 32], F32, 'ct2')
        LJr3 = mkap(LJre[:, 0:1], [[0, 2], [1, 32]])
        for si, (b0, nb) in enumerate(c.segblk):
            S.op('dve', lambda e, si=si, b0=b0: e.tensor_copy(out=Cin[:, :, :, b0], in_=cin0[:, :, :, si]), [cin0], [Cin])
            for j in range(nb - 1):
                cur = Cin[:, :, :, b0 + j]
                cur_sw = mkap(Cin[:, 1:2, 0:1, b0 + j:b0 + j + 1], [[-32 * NBLK, 2], [NBLK, 32]])
                self.s5_cmul('dve', ct[:], cur, cur_sw, LJr3, LJsg[:], ct2[:], [Cin, LJre, LJsg, ct, ct2], [ct, ct2])
                S.op('dve', lambda e, j=j, b0=b0: e.tensor_tensor(out=Cin[:, :, :, b0 + j + 1], in0=ct[:], in1=Send[:, :, :, b0 + j], op=ALU.add), [ct, Send], [Cin])
        T = {'wbtf': P.sb([128, 8, 2, 128], BF16, 'wbtf'), 'wbtb': P.sb([128, 8, 2, 128], BF16, 'wbtb'),
             'psb': Ring([P.ps([128, 512], F32, 'psb') for _ in range(3)])}
        psy = Ring([P.ps([128, 512], F32, 'psy') for _ in range(3)])
        pstr = Ring([P.ps([128, 512], F32, 'pstr') for _ in range(2)])
        wc = P.sb([128, 4, 8, 128], BF16, 'wc')
        t0w = P.sb([128, 8, 128], BF16, 't0w')
        Bst = P.sb([128, 2, 8, NSC], F32, 'Bst')
        Sbf = P.sb([128, 2, 8, NSC], BF16, 'Sbf')
        t1 = P.sb([128, 2, 8, NBLK], F32, 't1')
        t2 = P.sb([128, 2, 8, NBLK], F32, 't2')
        ysb = Ring([P.sb([128, 512], F32, 'ysb') for _ in range(3)])
        ytr = Ring([P.sb([128, 8, 8, 16], F32, 'ytr') for _ in range(2)])
        L8re, L8sg = Lc['L8']
        for g0 in range(0, 32, 8):
            self.s5_load_batch_w(P, T, l, g0)
            for k in range(4):
                S.dma(wc[:, k, :, :].rearrange("p g m -> p (g m)"), self.s5wc[l, k, :, g0 * 128:(g0 + 8) * 128], wc, self.s5wc)
            S.dma(t0w[:].rearrange("p g m -> p (g m)"), self.s5t0[l, :, g0 * 128:(g0 + 8) * 128], t0w, self.s5t0)
            self.s5_bstage(P, T, l, g0, U, Bst)
            Lre = mkap(L8re[:, g0:g0 + 1], [[0, 2], [1, 8], [0, NBLK]])
            Lsg = mkap(L8sg[:, 0:1, g0:g0 + 1], [[32, 2], [1, 8], [0, NBLK]])
            bcol = lambda r: mkap(Bst[:, 0:1, 0:1, r:r + 1], [[8 * NSC, 2], [NSC, 8], [J, NBLK]])
            bcol_sw = lambda r: mkap(Bst[:, 1:2, 0:1, r:r + 1], [[-8 * NSC, 2], [NSC, 8], [J, NBLK]])
            cin_b = Cin[:, :, g0:g0 + 8, :]
            cin_sw = mkap(Cin[:, 1:2, g0:g0 + 1, 0:1], [[-32 * NBLK, 2], [NBLK, 8], [1, NBLK]])
            self.s5_cmul('dve', t1[:], cin_b, cin_sw, Lre, Lsg, t2[:], [Cin, L8re, L8sg, t1, t2], [t1, t2])
            S.op('dve', lambda e, b0_=bcol(0): e.tensor_tensor(out=b0_, in0=b0_, in1=t1[:], op=ALU.add), [t1, Bst], [Bst])
            for r in range(1, J):
                self.s5_cmul('dve', t1[:], bcol(r - 1), bcol_sw(r - 1), Lre, Lsg, t2[:], [Bst, L8re, L8sg, t1, t2], [t1, t2])
                S.op('dve', lambda e, br=bcol(r): e.tensor_tensor(out=br, in0=br, in1=t1[:], op=ALU.add), [t1, Bst], [Bst])
            S.op('act', lambda e: e.activation(out=mkap(Sbf[:, 0:1, 0:1, 1:2], [[NSC, 16], [J, NBLK], [1, J - 1]]),
                                               in_=mkap(Bst[:, 0:1, 0:1, 0:1], [[NSC, 16], [J, NBLK], [1, J - 1]]), func=AF.Copy), [Bst], [Sbf])
            S.op('pool', lambda e, cin_b=cin_b: e.tensor_copy(out=mkap(Sbf[:, 0:1, 0:1, 0:1], [[8 * NSC, 2], [NSC, 8], [J, NBLK]]), in_=cin_b), [Cin], [Sbf])
            for si, (T0, N, R) in enumerate(c.segs):
                c0, ns = T0 // 8, N // 8
                for cc in range(0, ns, 128):
                    w = min(128, ns - cc)
                    yt = ytr.next()
                    for gl in range(8):
                        g = g0 + gl
                        py = psy.next()
                        nat = lambda t_, ri_: t_[:, ri_, gl, c0 + cc:c0 + cc + w]
                        rev = lambda t_, ri_: mkap(t_[:, ri_, gl, c0 + ns - 1 - cc:c0 + ns - cc], [[-1, w]])
                        ops = [(wc[:, 0, gl, :], nat(Sbf, 0)), (wc[:, 1, gl, :], nat(Sbf, 1)), (wc[:, 2, gl, :], rev(Sbf, 0)), (wc[:, 3, gl, :], rev(Sbf, 1)),
                               (t0w[:, gl, :], U[:, g, c0 + cc:c0 + cc + w])]
                        for i, (lh, rh) in enumerate(ops):
                            S.op('pe', lambda e, py=py, lh=lh, rh=rh, i=i, w=w: e.matmul(py[:, 0:w], lh, rh, start=(i == 0), stop=(i == 4)),
                                 [wc, t0w, Sbf, U], [py], inc=(i == 4))
                        ys = ysb.next()
                        S.op('act', lambda e, py=py, ys=ys, w=w: e.activation(out=ys[:, 0:w], in_=py[:, 0:w], func=AF.Copy), [py], [ys])
                        ptr = pstr.next()
                        S.op('pe', lambda e, ys=ys, ptr=ptr, w=w: e.transpose(out=ptr[0:w, 0:128], in_=ys[:, 0:w], identity=K['identf'][:]), [ys, K['identf']], [ptr])
                        S.op('dve', lambda e, ptr=ptr, yt=yt, gl=gl, w=w: e.tensor_copy(out=yt[0:w, :, gl, :], in_=ptr[0:w, 0:128].rearrange("p (t h) -> p t h", t=8)),
                             [ptr], [yt])
                    tok0 = T0 + cc * 8
                    S.dma(self.ystok[tok0:tok0 + w * 8, g0 * 16:(g0 + 8) * 16].rearrange("(c t) f -> c t f", t=8),
                          yt[0:w, :, :, :].rearrange("p t g h -> p t (g h)"), self.ystok, yt, q='pool')
                S.emit()
        P.close()

    def mix_s5_c(self, l):
        c, S = self.cfg, self.S
        P = Phase(self, 's5C%d' % l)
        K = self.load_consts(P, want=('identf', 'identb', 'pvec'))
        wg = P.sb([128, 4, 4, 128], BF16, 'wg')
        S.dma(wg[:], self.wb['glu'][l * 4:(l + 1) * 4, :, :].rearrange("m p (k n) -> p m k n", k=4), wg, self.wb['glu'])
        ysr = Ring([P.sb([128, 4, 512], F32, 'ys') for _ in range(2)])
        xsr = Ring([P.sb([128, 4, 512], BF16, 'xs') for _ in range(2)])
        yv = Ring([P.sb([128, 4, 512], F32, 'yv') for _ in range(2)])
        yb = Ring([P.sb([128, 4, 512], BF16, 'yb') for _ in range(2)])
        sg = Ring([P.sb([128, 512], F32, 'sg') for _ in range(2)])
        yo = Ring([P.sb([128, 512], F32, 'yo') for _ in range(3)])
        psa = Ring([P.ps([128, 512], F32, 'psa') for _ in range(3)])
        psx = Ring([P.ps([128, 1024], BF16, 'psx') for _ in range(2)])
        psg = Ring([P.ps([128, 512], F32, 'psg') for _ in range(2)])
        do, _ = self.lay['s5_d%d' % l]
        bo, _ = self.lay['b_glu%d' % l]
        for t0 in range(0, c.NT, 512):
            ys, xs, y, ybf = ysr.next(), xsr.next(), yv.next(), yb.next()
            S.dma(ys[:], self.ystok[t0:t0 + 512, :].rearrange("(b p) f -> p b f", p=128), ys, self.ystok)
            S.dma(xs[:], self.xstok[t0:t0 + 512, :].rearrange("(b p) f -> p b f", p=128), xs, self.xstok)
            for fc in range(4):
                pa = psa.next()
                px = psx.next()
                for tb in range(4):
                    S.op('pe', lambda e, pa=pa, tb=tb, fc=fc, ys=ys: e.transpose(out=pa[:, tb * 128:(tb + 1) * 128], in_=ys[:, tb, fc * 128:(fc + 1) * 128],
                                                                              identity=K['identf'][:]), [ys, K['identf']], [pa], inc=(tb == 3))
                for tb in range(4):
                    S.op('pe', lambda e, px=px, tb=tb, fc=fc, xs=xs: e.transpose(out=px[:, tb * 128:(tb + 1) * 128], in_=xs[:, tb, fc * 128:(fc + 1) * 128],
                                                                              identity=K['identb'][:]), [xs, K['identb']], [px], inc=(tb == 3))
                S.op('act', lambda e, pa=pa, y=y, fc=fc: e.activation(out=y[:, fc, :], in_=pa[:], func=AF.Copy), [pa], [y])
                S.op('dve', lambda e, px=px, y=y, fc=fc: e.scalar_tensor_tensor(out=y[:, fc, :], in0=px[:, 0:512], scalar=K['pvec'][:, do + fc:do + fc + 1],
                                                                              in1=y[:, fc, :], op0=ALU.mult, op1=ALU.add), [px, y, K['pvec']], [y])
                S.op('act', lambda e, y=y, fc=fc: e.activation(out=y[:, fc, :], in_=y[:, fc, :], func=AF.Gelu_apprx_tanh), [y], [y])
                S.op('pool', lambda e, y=y, ybf=ybf, fc=fc: e.tensor_copy(out=ybf[:, fc, :], in_=y[:, fc, :]), [y], [ybf])
            for mc in range(4):
                pg = psg.next()
                for kc in range(4):
                    S.op('pe', lambda e, pg=pg, mc=mc, kc=kc, ybf=ybf: e.matmul(pg[:], wg[:, mc, kc, :], ybf[:, kc, :], start=(kc == 0), stop=(kc == 3)),
                         [wg, ybf], [pg], inc=(kc == 3))
                s_ = sg.next()
                o_ = yo.next()
                S.op('act', lambda e, pg=pg, s_=s_, mc=mc: e.activation(out=s_[:], in_=pg[:], func=AF.Sigmoid, bias=K['pvec'][:, bo + mc:bo + mc + 1]),
                     [pg, K['pvec']], [s_])
                S.op('dve', lambda e, s_=s_, o_=o_, y=y, mc=mc: e.tensor_tensor(out=o_[:], in0=y[:, mc, :], in1=s_[:], op=ALU.mult), [s_, y], [o_])
                S.dma(self.ymixraw[1536 + mc * 128:1536 + (mc + 1) * 128, t0:t0 + 512], o_[:], self.ymixraw, o_, q='pool')
            S.emit()
        P.close()

    def mix_norm(self, l):
        c, S = self.cfg, self.S
        P = Phase(self, 'mnorm%d' % l)
        T = {'K': self.load_consts(P, want=('ones', 'pvec'))}
        T['rstd'] = P.sb([128, 512], F32, 'rstd')
        T['tmp'] = P.sb([128, 512], F32, 'tmp')
        T['sq'] = Ring([P.sb([128, 512], BF16, 'sq') for _ in range(2)])
        T['psn'] = Ring([P.ps([128, 512], F32, 'psn') for _ in range(2)])
        xin = Ring([P.sb([128, 8, 512], F32, 'xin') for _ in range(3)])
        xo = Ring([P.sb([128, 8, 512], BF16, 'xo') for _ in range(3)])
        for t0 in range(0, c.NT, 512):
            for (r0, nch) in ((0, 8), (1024, 4), (1536, 4)):
                xi, xb = xin.next(), xo.next()
                S.dma(xi[:, 0:nch, :], self.ymixraw[r0:r0 + nch * 128, t0:t0 + 512].rearrange("(k p) t -> p k t", p=128), xi, self.ymixraw)
                self.rmsnorm_fm(P, T, xi, nch, 'g_out%d' % l, xb, 512, goff=r0 // 128)
                S.dma(self.ymixT[r0:r0 + nch * 128, t0:t0 + 512].rearrange("(k p) t -> p k t", p=128), xb[:, 0:nch, :], self.ymixT, xb, q='pool')
            S.emit()
        P.close()

    def mix_lru(self, l):
        pass


    def mixers(self, l):
        if self.stub_mixer:
            S = self.S
            P = Phase(self, 'mixstub%d' % l)
            S.dma(self.ymixT[:, :], self.zT[0:2048, :], self.ymixT, self.zT)
            P.close()
            return
        self.mix_exchange1(l)
        self.mix_na(l)
        self.mix_lru_a(l)
        self.mix_s5_a(l)
        self.mix_exchange2(l)
        self.mix_lru_b(l)
        self.mix_s5_b(l)
        self.mix_s5_c(l)
        self.mix_norm(l)

    def build(self):
        c = self.cfg
        self.declare()
        self.prologue()
        if not self.stub_mixer:
            self.prologue_tables()
        self.dense(None, 0)
        for l in range(c.L):
            self.mixers(l)
            self.dense(l, l + 1 if l + 1 < c.L else None)
        return self.nc


def run_cfg(cfg, inputs, debug=(), stub_mixer=False, trace=False):
    in_maps = host_layout(cfg, inputs)
    K = Kern(cfg, debug=debug, stub_mixer=stub_mixer)
    nc = K.build()
    with K.es:
        pass
    res = run_bass_kernel_spmd(nc, in_maps, core_ids=list(range(NCORES)), trace=trace)
    return res, K


def assemble(cfg, results):
    B_s, B_p = 2, 4
    ys = np.zeros((B_s, cfg.SEQ_S, cfg.D), np.float32)
    yp = np.zeros((B_p, cfg.SEQ_P, cfg.D), np.float32)
    for c in range(NCORES):
        y = np.asarray(results[c]['y_tok'])
        sq, qq = c // 4, c % 4
        pq, hh = c // 2, c % 2
        ys[sq, qq * cfg.NS:(qq + 1) * cfg.NS] = y[:cfg.NS]
        yp[pq, hh * cfg.NP:(hh + 1) * cfg.NP] = y[cfg.NS:]
    return yp, ys


def kernel(**inputs):
    cfg = Cfg()
    res, _ = run_cfg(cfg, inputs)
    return assemble(cfg, res.results)
```
